# Optimizing a Trainium2 kernel written in Bass

```python
import math
import jax, jax.numpy as jnp
from jax import lax
import numpy as np

D_MODEL = 1024
BATCH = 1
SEQ = 16384
DEPTH = 2
DEC_BATCH = 8
DEC_SEQ = 8192
PAST_LEN = 128

EPS = 1e-6
N_MEM = 256

SSM_HEADS = 8
SSM_HEAD_DIM = 64
SSM_D = SSM_HEADS * SSM_HEAD_DIM
SSM_GROUPS = 2
SSM_HPG = SSM_HEADS // SSM_GROUPS
SSM_STATE = 128
D_CONV = 5
CONV_CH = SSM_D + 2 * SSM_GROUPS * SSM_STATE
CHUNK = 128
DT_MIN = 0.001
DT_MAX = 0.1

MLA_HEADS = 8
QK_NOPE = 64
QK_ROPE = 32
V_DIM = 64
Q_LORA = 256
KV_LORA = 128
ROPE_THETA = 10000.0
Q_BLOCK = 128
MLA_D = MLA_HEADS * V_DIM

D_IN_EVEN = SSM_D + CONV_CH + 2 * SSM_HEADS + Q_LORA + KV_LORA + QK_ROPE
D_MIX_EVEN = SSM_D + MLA_D

FOURIER_GROUPS = 4
FOURIER_GROUP_DIM = D_MODEL // FOURIER_GROUPS

XA_HEADS = 4
XA_HEAD_DIM = D_MODEL // XA_HEADS

D_FF = ((8 * D_MODEL + 3 * 256 - 1) // (3 * 256)) * 256

kernel_name = 'hybrid_ssd_mla_fnet_encoder'


def rmsnorm(x, w):
    xf = x.astype(jnp.float32)
    y = xf * lax.rsqrt(jnp.mean(xf * xf, axis=-1, keepdims=True) + EPS)
    return (y * w.astype(jnp.float32)).astype(x.dtype)


def rope_tables(s):
    inv = ROPE_THETA ** (-jnp.arange(0, QK_ROPE, 2, dtype=jnp.float32) / QK_ROPE)
    ang = jnp.arange(s, dtype=jnp.float32)[:, None] * inv[None, :]
    return jnp.cos(ang), jnp.sin(ang)


def apply_rope(x, cos, sin):
    half = x.shape[-1] // 2
    x1, x2 = x[..., :half], x[..., half:]
    return jnp.concatenate([x1 * cos - x2 * sin, x2 * cos + x1 * sin], axis=-1).astype(x.dtype)


def depthwise_conv(x, w, b):
    y = lax.conv_general_dilated(
        x, w[:, None, :].astype(x.dtype), window_strides=(1,),
        padding=[(D_CONV // 2, D_CONV // 2)],
        dimension_numbers=('NWC', 'WIO', 'NWC'),
        feature_group_count=x.shape[-1])
    return y + b.astype(x.dtype)


def ssd_chunked(x, dt, a, bm, cm):
    b, l = x.shape[0], x.shape[1]
    c = l // CHUNK
    xc = x.reshape(b, c, CHUNK, SSM_GROUPS, SSM_HPG, SSM_HEAD_DIM)
    dtc = dt.reshape(b, c, CHUNK, SSM_GROUPS, SSM_HPG)
    bc = bm.reshape(b, c, CHUNK, SSM_GROUPS, SSM_STATE)
    cc = cm.reshape(b, c, CHUNK, SSM_GROUPS, SSM_STATE)
    acs = jnp.cumsum(dtc * a, axis=2)
    xdt = xc * dtc[..., None]
    mask = jnp.tril(jnp.ones((CHUNK, CHUNK), dtype=bool))[:, :, None, None]
    seg = acs[:, :, :, None] - acs[:, :, None, :]
    decay = jnp.exp(jnp.where(mask, seg, -jnp.inf))
    cb = jnp.einsum('bclgn,bcsgn->bclsg', cc, bc)
    y_diag = jnp.einsum('bclsgr,bcsgrp->bclgrp', cb[..., None] * decay, xdt)
    decay_st = jnp.exp(acs[:, :, -1:] - acs)
    states = jnp.einsum('bclgn,bclgrp->bcgrpn', bc, xdt * decay_st[..., None])
    chunk_decay = jnp.exp(acs[:, :, -1])

    def step(h, inp):
        st, dec = inp
        return h * dec[..., None, None] + st, h

    h0 = jnp.zeros((b, SSM_GROUPS, SSM_HPG, SSM_HEAD_DIM, SSM_STATE), jnp.float32)
    _, prev = lax.scan(step, h0, (jnp.moveaxis(states, 1, 0), jnp.moveaxis(chunk_decay, 1, 0)))
    prev = jnp.moveaxis(prev, 0, 1)
    y_off = jnp.einsum('bclgn,bcgrpn->bclgrp', cc, prev) * jnp.exp(acs)[..., None]
    return (y_diag + y_off).reshape(b, l, SSM_GROUPS, SSM_HPG, SSM_HEAD_DIM)


def block_attention(q, k, v, scale):
    b, s, h, dk = q.shape
    nb = s // Q_BLOCK
    qb = jnp.moveaxis(q.reshape(b, nb, Q_BLOCK, h, dk), 1, 0)

    def attend(qblk):
        sc = jnp.einsum('bqhd,bkhd->bhqk', qblk, k).astype(jnp.float32) * scale
        p = jax.nn.softmax(sc, axis=-1).astype(v.dtype)
        return jnp.einsum('bhqk,bkhd->bqhd', p, v)

    o = lax.map(attend, qb)
    return jnp.moveaxis(o, 0, 1).reshape(b, s, h, v.shape[-1])


def even_mixer(hn, w_in, conv_w, conv_b, a_log_f, a_log_b, dt_bias_f, dt_bias_b, d_skip,
               ssm_norm_w, q_norm_w, w_uq, kv_norm_w, w_ukv, w_out):
    b, s, _ = hn.shape
    f32 = jnp.float32
    proj = hn @ w_in
    o1 = SSM_D
    o2 = o1 + CONV_CH
    o3 = o2 + 2 * SSM_HEADS
    o4 = o3 + Q_LORA
    o5 = o4 + KV_LORA
    z, xbc, dt_raw, c_q, c_kv, k_r = jnp.split(proj, [o1, o2, o3, o4, o5], axis=-1)

    xbc = jax.nn.silu(depthwise_conv(xbc, conv_w, conv_b))
    xs, bm, cm = jnp.split(xbc, [SSM_D, SSM_D + SSM_GROUPS * SSM_STATE], axis=-1)
    xs = xs.reshape(b, s, SSM_GROUPS, SSM_HPG, SSM_HEAD_DIM).astype(f32)
    bm = bm.reshape(b, s, SSM_GROUPS, SSM_STATE).astype(f32)
    cm = cm.reshape(b, s, SSM_GROUPS, SSM_STATE).astype(f32)
    dt_raw = dt_raw.astype(f32)
    dt_f = jax.nn.softplus(dt_raw[..., :SSM_HEADS] + dt_bias_f.astype(f32)).reshape(b, s, SSM_GROUPS, SSM_HPG)
    dt_b = jax.nn.softplus(dt_raw[..., SSM_HEADS:] + dt_bias_b.astype(f32)).reshape(b, s, SSM_GROUPS, SSM_HPG)
    a_f = -jnp.exp(a_log_f.astype(f32)).reshape(SSM_GROUPS, SSM_HPG)
    a_b = -jnp.exp(a_log_b.astype(f32)).reshape(SSM_GROUPS, SSM_HPG)
    flip = lambda t: jnp.flip(t, axis=1)
    y_f = ssd_chunked(xs, dt_f, a_f, bm, cm)
    y_b = flip(ssd_chunked(flip(xs), flip(dt_b), a_b, flip(bm), flip(cm)))
    y = y_f + y_b + xs * d_skip.astype(f32).reshape(SSM_GROUPS, SSM_HPG)[..., None]
    y = y.reshape(b, s, SSM_D) * jax.nn.silu(z.astype(f32))
    yg = y.reshape(b, s, SSM_GROUPS, SSM_D // SSM_GROUPS)
    yg = yg * lax.rsqrt(jnp.mean(yg * yg, axis=-1, keepdims=True) + EPS)
    y_ssd = (yg.reshape(b, s, SSM_D) * ssm_norm_w.astype(f32)).astype(hn.dtype)

    cos, sin = rope_tables(s)
    q = (rmsnorm(c_q, q_norm_w) @ w_uq).reshape(b, s, MLA_HEADS, QK_NOPE + QK_ROPE)
    q_nope, q_rope = q[..., :QK_NOPE], q[..., QK_NOPE:]
    q_rope = apply_rope(q_rope, cos[None, :, None], sin[None, :, None])
    kv = (rmsnorm(c_kv, kv_norm_w) @ w_ukv).reshape(b, s, MLA_HEADS, QK_NOPE + V_DIM)
    k_nope, v = kv[..., :QK_NOPE], kv[..., QK_NOPE:]
    k_r = apply_rope(k_r, cos[None], sin[None])
    q_full = jnp.concatenate([q_nope, q_rope], axis=-1)
    k_full = jnp.concatenate(
        [k_nope, jnp.broadcast_to(k_r[:, :, None], (b, s, MLA_HEADS, QK_ROPE))], axis=-1)
    o = block_attention(q_full, k_full, v, (QK_NOPE + QK_ROPE) ** -0.5)

    mixed = jnp.concatenate([y_ssd, o.reshape(b, s, MLA_D).astype(hn.dtype)], axis=-1)
    return mixed @ w_out


def fourier_mixer(hn, w_mix):
    b, s, _ = hn.shape
    xg = hn.astype(jnp.float32).reshape(b, s, FOURIER_GROUPS, FOURIER_GROUP_DIM)
    f = jnp.fft.fftn(xg, axes=(1, 3), norm='ortho').real
    return f.reshape(b, s, D_MODEL).astype(hn.dtype) @ w_mix


def cross_attention(hn, mem_n, wq, wkv, wo):
    b, s, _ = hn.shape
    m = mem_n.shape[1]
    q = (hn @ wq).reshape(b, s, XA_HEADS, XA_HEAD_DIM)
    k, v = jnp.split(mem_n @ wkv, 2, axis=-1)
    k = k.reshape(b, m, XA_HEADS, XA_HEAD_DIM)
    v = v.reshape(b, m, XA_HEADS, XA_HEAD_DIM)
    sc = jnp.einsum('bshd,bmhd->bhsm', q, k).astype(jnp.float32) * (XA_HEAD_DIM ** -0.5)
    p = jax.nn.softmax(sc, axis=-1).astype(v.dtype)
    o = jnp.einsum('bhsm,bmhd->bshd', p, v).reshape(b, s, D_MODEL)
    return o @ wo


def swiglu(hn, w_gu, w_down):
    g, u = jnp.split(hn @ w_gu, 2, axis=-1)
    return (jax.nn.silu(g) * u) @ w_down


def trunk(x, mem, layer_p, even_p, od_w_mix):
    (norm_mix_pre, norm_mix_post, norm_xa_pre, norm_xa_post, norm_mem, xa_wq, xa_wkv, xa_wo,
     norm_ffn_pre, norm_ffn_post, ffn_w_gu, ffn_w_down) = layer_p
    h = x
    for i in range(DEPTH):
        hn = rmsnorm(h, norm_mix_pre[i])
        if i % 2 == 0:
            mix = even_mixer(hn, *[p[i // 2] for p in even_p])
        else:
            mix = fourier_mixer(hn, od_w_mix[i // 2])
        h = h + rmsnorm(mix, norm_mix_post[i])
        mem_n = rmsnorm(mem, norm_mem[i])
        xa = cross_attention(rmsnorm(h, norm_xa_pre[i]), mem_n, xa_wq[i], xa_wkv[i], xa_wo[i])
        h = h + rmsnorm(xa, norm_xa_post[i])
        ff = swiglu(rmsnorm(h, norm_ffn_pre[i]), ffn_w_gu[i], ffn_w_down[i])
        h = h + rmsnorm(ff, norm_ffn_post[i])
    return h


def setup_inputs(seed: int = 0) -> dict:
    key = jax.random.key(seed)
    ks = iter(jax.random.split(key, 64))
    f32 = jnp.float32
    n_even = (DEPTH + 1) // 2
    n_odd = DEPTH // 2
    d = D_MODEL

    def nrm(shape, scale):
        return scale * jax.random.normal(next(ks), shape, f32)

    def gain(shape):
        return 1.0 + 0.05 * jax.random.normal(next(ks), shape, f32)

    def dt_bias(shape):
        u = jax.random.uniform(next(ks), shape, f32)
        dt = jnp.exp(u * (math.log(DT_MAX) - math.log(DT_MIN)) + math.log(DT_MIN))
        return dt + jnp.log(-jnp.expm1(-dt))

    def a_log(shape):
        return jnp.log(jax.random.uniform(next(ks), shape, f32, 1.0, 16.0))

    return {
        'x_prompt': nrm((BATCH, SEQ, d), 1.0),
        'x_sample': nrm((DEC_BATCH, DEC_SEQ, d), 1.0),
        'mem_prompt': nrm((BATCH, N_MEM, d), 1.0),
        'mem_sample': nrm((DEC_BATCH, N_MEM, d), 1.0),
        'norm_mix_pre': gain((DEPTH, d)),
        'norm_mix_post': gain((DEPTH, d)),
        'norm_xa_pre': gain((DEPTH, d)),
        'norm_xa_post': gain((DEPTH, d)),
        'norm_mem': gain((DEPTH, d)),
        'xa_wq': nrm((DEPTH, d, d), d ** -0.5),
        'xa_wkv': nrm((DEPTH, d, 2 * d), d ** -0.5),
        'xa_wo': nrm((DEPTH, d, d), d ** -0.5),
        'norm_ffn_pre': gain((DEPTH, d)),
        'norm_ffn_post': gain((DEPTH, d)),
        'ffn_w_gu': nrm((DEPTH, d, 2 * D_FF), d ** -0.5),
        'ffn_w_down': nrm((DEPTH, D_FF, d), D_FF ** -0.5),
        'ev_w_in': nrm((n_even, d, D_IN_EVEN), d ** -0.5),
        'ev_conv_w': nrm((n_even, D_CONV, CONV_CH), D_CONV ** -0.5),
        'ev_conv_b': nrm((n_even, CONV_CH), 0.02),
        'ev_a_log_f': a_log((n_even, SSM_HEADS)),
        'ev_a_log_b': a_log((n_even, SSM_HEADS)),
        'ev_dt_bias_f': dt_bias((n_even, SSM_HEADS)),
        'ev_dt_bias_b': dt_bias((n_even, SSM_HEADS)),
        'ev_d_skip': gain((n_even, SSM_HEADS)),
        'ev_ssm_norm': gain((n_even, SSM_D)),
        'ev_q_norm': gain((n_even, Q_LORA)),
        'ev_w_uq': nrm((n_even, Q_LORA, MLA_HEADS * (QK_NOPE + QK_ROPE)), Q_LORA ** -0.5),
        'ev_kv_norm': gain((n_even, KV_LORA)),
        'ev_w_ukv': nrm((n_even, KV_LORA, MLA_HEADS * (QK_NOPE + V_DIM)), KV_LORA ** -0.5),
        'ev_w_out': nrm((n_even, D_MIX_EVEN, d), D_MIX_EVEN ** -0.5),
        'od_w_mix': nrm((n_odd, d, d), d ** -0.5),
    }


def reference(x_prompt, x_sample, mem_prompt, mem_sample,
              norm_mix_pre, norm_mix_post, norm_xa_pre, norm_xa_post, norm_mem,
              xa_wq, xa_wkv, xa_wo, norm_ffn_pre, norm_ffn_post, ffn_w_gu, ffn_w_down,
              ev_w_in, ev_conv_w, ev_conv_b, ev_a_log_f, ev_a_log_b, ev_dt_bias_f, ev_dt_bias_b,
              ev_d_skip, ev_ssm_norm, ev_q_norm, ev_w_uq, ev_kv_norm, ev_w_ukv, ev_w_out,
              od_w_mix):
    layer_p = (norm_mix_pre, norm_mix_post, norm_xa_pre, norm_xa_post, norm_mem, xa_wq, xa_wkv, xa_wo,
               norm_ffn_pre, norm_ffn_post, ffn_w_gu, ffn_w_down)
    even_p = (ev_w_in, ev_conv_w, ev_conv_b, ev_a_log_f, ev_a_log_b, ev_dt_bias_f, ev_dt_bias_b,
              ev_d_skip, ev_ssm_norm, ev_q_norm, ev_w_uq, ev_kv_norm, ev_w_ukv, ev_w_out)
    y_prompt = trunk(x_prompt, mem_prompt, layer_p, even_p, od_w_mix)
    y_sample = trunk(x_sample, mem_sample, layer_p, even_p, od_w_mix)
    return (y_prompt, y_sample)
```

```python
import contextlib, math
import numpy as np
import ml_dtypes
import concourse.bass as bass
import concourse.mybir as mybir
from concourse.bass_utils import run_bass_kernel_spmd

F32 = mybir.dt.float32
BF16 = mybir.dt.bfloat16
U8 = mybir.dt.uint8
AF = mybir.ActivationFunctionType
ALU = mybir.AluOpType
AX = mybir.AxisListType

D = 1024
NMEM = 256
DFF = 2816
DIN = 1968
EPS = 1e-6
ARENA = 200 * 1024
EP = 30000


class T:
    def __init__(self, ap):
        self.ap = ap
        self.w = None
        self.r = []

    def __getitem__(self, k):
        return self.ap[k]


class Node:
    __slots__ = ("eng", "fn", "deps", "sig", "dma", "semv", "nonc")

    def __init__(self, eng, fn, deps, dma=False):
        self.eng, self.fn, self.deps, self.dma = eng, fn, deps, dma
        self.sig = False
        self.semv = None
        self.nonc = False


class Em:
    ENGS = ["pe", "act", "dve", "pool", "sp"]

    def __init__(self, nc, es):
        self.nc, self.es = nc, es
        self.q = {e: [] for e in self.ENGS}
        self.dsem = {}
        self.npool = {"sp": 28, "pool": 20, "act": 8}
        for qn, n in self.npool.items():
            self.dsem[qn] = [[es.enter_context(nc.semaphore(f"d_{qn}{i}")), 0, None] for i in range(n)]
        self.drr = {qn: 0 for qn in self.npool}
        self.esem = {e: [] for e in self.ENGS}
        self.outstanding = []

    def _deps(self, reads, writes):
        deps = []
        for t in reads:
            if t.w is not None:
                deps.append(t.w)
        for t in writes:
            if t.w is not None:
                deps.append(t.w)
            deps.extend(t.r)
        return deps

    def _track(self, node, reads, writes):
        for t in reads:
            if not node.dma:
                t.r = [n for n in t.r if n.dma or n.eng != node.eng]
            t.r.append(node)
        for t in writes:
            t.w = node
            t.r = []
        for d in node.deps:
            if not (d.eng == "pe" and node.eng == "pe" and not d.dma and not node.dma):
                d.sig = True

    def op(self, eng, fn, reads=(), writes=()):
        node = Node(eng, fn, self._deps(reads, writes))
        self._track(node, reads, writes)
        self.q[eng].append(node)
        return node

    def dma(self, qn, out, in_, reads=(), writes=(), nonc=False):
        slot = self.dsem[qn][self.drr[qn]]
        self.drr[qn] = (self.drr[qn] + 1) % self.npool[qn]
        deps = self._deps(reads, writes)
        if slot[2] is not None:
            deps.append(slot[2])
        node = Node(qn, lambda e: e.dma_start(out=out, in_=in_), deps, dma=True)
        node.nonc = nonc
        slot[1] += 16
        node.semv = (slot[0], slot[1])
        slot[2] = node
        self._track(node, reads, writes)
        self.q[qn].append(node)
        self.outstanding.append(node)
        return node

    def barrier(self):
        last = [self.q[e][-1] for e in self.ENGS if self.q[e]]
        deps = [n for n in last if n.fn is not None or True] + self.outstanding
        for e in self.ENGS:
            node = Node(e, None, list(deps))
            for d in deps:
                d.sig = True
            self.q[e].append(node)
        self.outstanding = []

    def finalize(self, block):
        nc, es = self.nc, self.es
        for e in self.ENGS:
            cnt = 0
            for node in self.q[e]:
                if node.dma or not node.sig or node.fn is None:
                    continue
                cnt += 1
                ep = (cnt - 1) // EP
                while len(self.esem[e]) <= ep:
                    self.esem[e].append(es.enter_context(nc.semaphore(f"e_{e}{len(self.esem[e])}")))
                node.semv = (self.esem[e][ep], (cnt - 1) % EP + 1)
        for e in self.ENGS:
            prev = None
            for node in self.q[e]:
                if node.fn is None:
                    node.semv = prev
                elif node.semv is not None:
                    prev = node.semv
        q = self.q

        def emit(ename, eng):
            known = {}
            for node in q[ename]:
                for d in node.deps:
                    if d.semv is None:
                        continue
                    if d.eng == "pe" and ename == "pe" and not d.dma and not node.dma:
                        continue
                    sem, val = d.semv
                    key = id(sem)
                    if known.get(key, 0) >= val:
                        continue
                    eng.wait_ge(sem, val)
                    known[key] = val
                if node.fn is None:
                    continue
                if node.nonc:
                    with nc.allow_non_contiguous_dma(reason="small strided vector load"):
                        ins = node.fn(eng)
                else:
                    ins = node.fn(eng)
                if node.dma:
                    ins.then_inc(node.semv[0], 16)
                elif node.sig:
                    ins.then_inc(node.semv[0], 1)

        @block.tensor
        def _(eng):
            emit("pe", eng)

        @block.scalar
        def _(eng):
            emit("act", eng)

        @block.vector
        def _(eng):
            emit("dve", eng)

        @block.gpsimd
        def _(eng):
            emit("pool", eng)

        @block.sync
        def _(eng):
            emit("sp", eng)


class Arena:
    def __init__(self, ap, nbytes):
        self.ap, self.n, self.off = ap, nbytes, 0

    def reset(self):
        self.off = 0

    def alloc(self, free, dt, parts=128):
        esz = {F32: 4, BF16: 2}[dt]
        n0 = int(np.prod(free)) * esz
        n = (n0 + 63) // 64 * 64
        assert self.off + n <= self.n, (self.off, n, self.n)
        ap = self.ap[0:parts, self.off:self.off + n0].bitcast(dt)
        self.off += n
        if len(free) == 2:
            ap = ap.rearrange("p (a b) -> p a b", b=free[1])
        elif len(free) == 3:
            ap = ap.rearrange("p (a b c) -> p a b c", b=free[1], c=free[2])
        return T(ap)


def bc(ap, shape, axis):
    return ap.unsqueeze(axis).to_broadcast(list(shape))


class Prog:
    def __init__(self, S0, S1, NC):
        self.S0, self.S1, self.NC = S0, S1, NC
        self.ST = S0 + S1
        self.seqs = [(0, S0), (S0, S1)]
        assert S1 % NC == 0

    def mm(self, out, lhsT, rhs, start, stop, reads, writes):
        self.em.op("pe", lambda e: e.matmul(out, lhsT=lhsT, rhs=rhs, start=start, stop=stop), reads, writes)

    def tr(self, out, in_, reads, writes):
        idb = self.identb
        n = in_.shape[0]
        self.em.op("pe", lambda e: e.transpose(out=out, in_=in_, identity=idb[0:n, 0:n]), list(reads) + [self.c_ident], writes)

    def act(self, out, in_, func, reads, writes, bias=None, scale=1.0, accum=None):
        kw = {}
        if bias is not None:
            kw["bias"] = bias
        if accum is not None:
            kw["accum_out"] = accum
        self.em.op("act", lambda e: e.activation(out=out, in_=in_, func=func, scale=scale, **kw), reads, writes)

    def tt(self, out, in0, in1, op, reads, writes, eng="dve"):
        self.em.op(eng, lambda e: e.tensor_tensor(out=out, in0=in0, in1=in1, op=op), reads, writes)

    def ts(self, out, in0, s1, s2, op0, op1, reads, writes, eng="dve"):
        if s2 is None:
            self.em.op(eng, lambda e: e.tensor_scalar(out=out, in0=in0, scalar1=s1, scalar2=None, op0=op0), reads, writes)
        else:
            self.em.op(eng, lambda e: e.tensor_scalar(out=out, in0=in0, scalar1=s1, scalar2=s2, op0=op0, op1=op1), reads, writes)

    def stt(self, out, in0, scalar, in1, op0, op1, reads, writes, eng="dve"):
        self.em.op(eng, lambda e: e.scalar_tensor_tensor(out=out, in0=in0, scalar=scalar, in1=in1, op0=op0, op1=op1), reads, writes)

    def cp(self, out, in_, reads, writes, eng="dve"):
        if eng == "act":
            eng = "dve"
        if False:
            pass
        else:
            self.em.op(eng, lambda e: e.tensor_copy(out=out, in_=in_), reads, writes)

    def ld(self, t, out, in_, q="sp", nonc=False):
        self.em.dma(q, out, in_, reads=(), writes=[t], nonc=nonc)

    def stq(self, t, out, in_, q="sp"):
        self.em.dma(q, out, in_, reads=[t], writes=())

    def rstd_from_ss(self, ss, rs, n, dim):
        self.act(rs[:, 0:n], ss[:, 0:n], AF.Ln, [ss, self.c_eps], [rs], bias=self.epsb[:, 0:1], scale=1.0 / dim)
        self.act(rs[:, 0:n], rs[:, 0:n], AF.Exp, [rs], [rs], scale=-0.5)

    def load_w(self, A, w_ap, K, N, name=None):
        kc = (K + 127) // 128
        t = A.alloc([kc, N], BF16)
        if K % 128 == 0:
            src = w_ap.rearrange("(k p) n -> p k n", p=128)
            for k in range(kc):
                for n0 in range(0, N, 2048):
                    n1 = min(N, n0 + 2048)
                    self.em.dma("pool", t[:, k, n0:n1], src[:, k, n0:n1], writes=[t])
        else:
            assert K < 128
            self.em.dma("pool", t[0:K, 0, :], w_ap, writes=[t])
        return t

    def load_vec_fm(self, A, v_ap, n):
        t = A.alloc([n // 128], F32)
        self.ld(t, t[:, :], v_ap.rearrange("(k p) -> p k", p=128), nonc=True)
        return t

    def load_vec_bc(self, A, v_ap, n):
        t = A.alloc([n], F32)
        self.ld(t, t[:, :], v_ap.partition_broadcast(128))
        return t

    def norm_T(self, A, x, xin_reads, dim, wfm, outT, psb, tmp_b, ss, rs, junk):
        kc = dim // 128
        self.act(junk[:, 0:dim], x, AF.Square, xin_reads, [junk, ss], accum=ss[:, 0:1])
        self.rstd_from_ss(ss, rs, 1, dim)
        self.ts(tmp_b[:, 0:dim], x, rs[:, 0:1], None, ALU.mult, None, list(xin_reads) + [rs], [tmp_b])
        pv = psb.ap.bitcast(BF16)
        for k in range(kc):
            self.tr(pv[:, k * 128:(k + 1) * 128], tmp_b[:, k * 128:(k + 1) * 128], [tmp_b], [psb])
        return pv

    def build(self):
        S0, S1, ST, NC = self.S0, self.S1, self.ST, self.NC
        nc = bass.Bass("TRN2", target_bir_lowering=False)
        self.nc = nc
        dp = {}

        def din(name, shape, dt=F32):
            dp[name] = nc.dram_tensor(name, list(shape), dt, kind="ExternalInput").ap()
            return dp[name]

        def dscr(name, shape, dt):
            dp[name] = nc.dram_tensor(name, list(shape), dt).ap()
            return dp[name]

        din("x", [ST, D]); din("mem", [2, NMEM, D])
        for nm in ["norm_mix_pre", "norm_mix_post", "norm_xa_pre", "norm_xa_post", "norm_mem", "norm_ffn_pre", "norm_ffn_post"]:
            din(nm, [2, D])
        din("xa_wq", [2, D, D]); din("xa_wkv", [2, D, 2 * D]); din("xa_wo", [2, D, D])
        din("ffn_w_gu", [2, D, 2 * DFF]); din("ffn_w_down", [2, DFF, D])
        din("ev_w_in", [1, D, DIN]); din("ev_conv_w", [1, 5, D]); din("ev_conv_b", [1, D])
        for nm in ["ev_a_log_f", "ev_a_log_b", "ev_dt_bias_f", "ev_dt_bias_b", "ev_d_skip"]:
            din(nm, [1, 8])
        din("ev_ssm_norm", [1, 512]); din("ev_q_norm", [1, 256]); din("ev_w_uq", [1, 256, 768])
        din("ev_kv_norm", [1, 128]); din("ev_w_ukv", [1, 128, 1024]); din("ev_w_out", [1, D, D])
        din("od_w_mix", [1, D, D])
        din("c_ident", [128, 128], BF16); din("c_tri", [4, 128, 128]); din("c_ones", [128, 128]); din("c_idf", [128, 128])
        din("c_rope", [max(S0, S1), 32]); din("c_cs", [256, 512])
        for i, S in enumerate([S0, S1]):
            NB = S // 128
            din(f"c_d1_{i}", [2, NB, 2 * NB]); din(f"c_e_{i}", [128, NB, 2, 128], BF16)
        yo0 = nc.dram_tensor("y0", [S0, D], F32, kind="ExternalOutput").ap()
        yo1 = nc.dram_tensor("y1", [S1, D], F32, kind="ExternalOutput").ap()
        dscr("h", [ST, D], F32); dscr("z", [ST, 512], F32); dscr("xbcT", [D, ST + 4], F32)
        dscr("dtp", [ST, 32], F32); dscr("cq", [ST, 256], F32); dscr("ckv", [ST, 160], F32)
        dscr("xtm", [ST, 512], BF16); dscr("btm", [ST, 256], BF16); dscr("BT", [256, ST], BF16); dscr("CT", [256, ST], BF16)
        dscr("Hst", [2, ST // 128, 128, 512], BF16)
        dscr("QT", [8, 128, ST], BF16); dscr("KT", [8, 128, ST], BF16); dscr("V", [ST, 8, 128], BF16)
        dscr("mixT", [D, ST], BF16)
        dscr("Zr", [ST, D], BF16); dscr("Zi", [ST, D], BF16); dscr("f", [ST, D], BF16)
        self.dp = dp

        with contextlib.ExitStack() as es:
            arena = es.enter_context(nc.sbuf_tensor("arena", [128, ARENA], U8))
            cst = es.enter_context(nc.sbuf_tensor("cst", [128, 6 * 1024], U8))
            psall = es.enter_context(nc.psum_tensor("psall", [128, 8, 512], F32))
            self.psall = psall
            ps = [T(psall[:, i, :]) for i in range(8)]
            self.ps = ps
            em = Em(nc, es)
            self.em = em
            A = Arena(arena, ARENA)
            self.A = A
            C = Arena(cst, 6 * 1024)
            self.c_ident = C.alloc([128], BF16); self.identb = self.c_ident.ap
            self.ld(self.c_ident, self.identb, dp["c_ident"])
            self.c_eps = C.alloc([1], F32); self.epsb = self.c_eps.ap
            em.op("dve", lambda e: e.memset(self.epsb, EPS), (), [self.c_eps])
            self.c_tri = C.alloc([4, 128], F32)
            self.ld(self.c_tri, self.c_tri[:, :, :], dp["c_tri"].rearrange("a p n -> p a n"))
            self.c_ones = C.alloc([128], F32)
            self.ld(self.c_ones, self.c_ones[:, :], dp["c_ones"])
            self.c_idf = C.alloc([128], F32); self.idf = self.c_idf.ap
            self.ld(self.c_idf, self.idf, dp["c_idf"])
            self.c_onesb = C.alloc([128], BF16)
            self.cp(self.c_onesb[:, :], self.c_ones[:, :], [self.c_ones], [self.c_onesb])

            stages = [self.stage_A, self.stage_conv, self.stage_mla_prep, self.stage_ssd1, self.stage_ssd2, self.stage_attn,
                      lambda: self.stage_mixout(0), lambda: self.stage_ffn(0, None, None), self.stage_F, self.stage_dft,
                      lambda: self.stage_mixout(1), lambda: self.stage_ffn(1, yo0, yo1)]
            for st_ in stages[:getattr(self, "nstages", 12)]:
                st_()
            em.barrier()
            block = es.enter_context(nc.Block())
            em.finalize(block)
        return nc

    def stage_A(self):
        em, A, dp, ps = self.em, self.A, self.dp, self.ps
        A.reset()
        w_in = self.load_w(A, dp["ev_w_in"][0], D, DIN)
        wpre = self.load_vec_fm(A, dp["norm_mix_pre"][0], D)
        dtb = A.alloc([16], F32)
        self.ld(dtb, dtb[:, 0:8], dp["ev_dt_bias_f"][0].partition_broadcast(128))
        self.ld(dtb, dtb[:, 8:16], dp["ev_dt_bias_b"][0].partition_broadcast(128))
        alog = A.alloc([16], F32)
        self.ld(alog, alog[:, 0:8], dp["ev_a_log_f"][0].partition_broadcast(128))
        self.ld(alog, alog[:, 8:16], dp["ev_a_log_b"][0].partition_broadcast(128))
        aneg = A.alloc([16], F32)
        self.act(aneg[:, :], alog[:, :], AF.Exp, [alog], [aneg])
        self.ts(aneg[:, :], aneg[:, :], -1.0, None, ALU.mult, None, [aneg], [aneg])
        NB_ = 2
        xt = [A.alloc([D], F32) for _ in range(NB_)]
        xb = [A.alloc([D], BF16) for _ in range(NB_)]
        junk = A.alloc([D], F32)
        xT = [A.alloc([8, 128], BF16) for _ in range(NB_)]
        ss = [A.alloc([2], F32) for _ in range(NB_)]
        rs = [A.alloc([2], F32) for _ in range(NB_)]
        fo = [A.alloc([128], F32) for _ in range(4)]
        zo = [A.alloc([512], F32) for _ in range(NB_)]
        so = [A.alloc([416], F32) for _ in range(NB_)]
        dto = [A.alloc([32], F32) for _ in range(NB_)]
        dtt = [A.alloc([16], F32) for _ in range(NB_)]
        nt = self.ST // 128
        fi = 0
        for i in range(nt):
            b = i % NB_
            r0 = i * 128
            self.ld(xt[b], xt[b][:, :], dp["x"][r0:r0 + 128, :])
            pv = self.norm_T(A, xt[b][:, :], [xt[b]], D, wpre, None, ps[0], xb[b], ss[b], rs[b], junk)
            self.tt(xT[b][:, :, :], pv[:, 0:1024].rearrange("p (k t) -> p k t", t=128), bc(wpre[:, :], [128, 8, 128], 2), ALU.mult,
                    [ps[0], wpre], [xT[b]])
            for k in range(8):
                self.mm(ps[1][:, :], xT[b][:, k, :], w_in[:, k, 0:512], k == 0, k == 7, [xT[b], w_in], [ps[1]])
            self.cp(zo[b][:, :], ps[1][:, :], [ps[1]], [zo[b]], eng="act" if False else "dve")
            self.stq(zo[b], dp["z"][r0:r0 + 128, :], zo[b][:, :])
            for k in range(8):
                self.mm(ps[2][:, 0:432], xT[b][:, k, :], w_in[:, k, 1536:1968], k == 0, k == 7, [xT[b], w_in], [ps[2]])
            self.cp(so[b][:, :], ps[2][:, 16:432], [ps[2]], [so[b]])
            self.stq(so[b], dp["cq"][r0:r0 + 128, :], so[b][:, 0:256])
            self.stq(so[b], dp["ckv"][r0:r0 + 128, :], so[b][:, 256:416])
            self.tt(dtt[b][:, :], ps[2][:, 0:16], dtb[:, :], ALU.add, [ps[2], dtb], [dtt[b]])
            self.act(dtt[b][:, :], dtt[b][:, :], AF.Exp, [dtt[b]], [dtt[b]])
            self.act(dto[b][:, 0:16], dtt[b][:, :], AF.Ln, [dtt[b], self.c_ones], [dto[b]], bias=self.c_ones[:, 0:1])
            self.tt(dto[b][:, 16:32], dto[b][:, 0:16], aneg[:, :], ALU.mult, [dto[b], aneg], [dto[b]])
            self.stq(dto[b], dp["dtp"][r0:r0 + 128, :], dto[b][:, :])
            for c in range(8):
                p = ps[3 + (c % 4)]
                for k in range(8):
                    self.mm(p[:, 0:128], w_in[:, k, 512 + c * 128:512 + (c + 1) * 128], xT[b][:, k, :], k == 0, k == 7, [xT[b], w_in], [p])
                f = fo[fi % 4]; fi += 1
                self.cp(f[:, :], p[:, 0:128], [p], [f], eng="act" if c % 2 else "dve")
                self.stq(f, dp["xbcT"][c * 128:(c + 1) * 128, 2 + r0:2 + r0 + 128], f[:, :])
        em.barrier()

    def stage_conv(self):
        em, A, dp, ps = self.em, self.A, self.dp, self.ps
        A.reset()
        cw = A.alloc([5, 8], F32)
        for k in range(5):
            self.ld(cw, cw[:, k, :], dp["ev_conv_w"][0, k].rearrange("(c p) -> p c", p=128), nonc=True)
        cb = self.load_vec_fm(A, dp["ev_conv_b"][0], D)
        TB = 512
        NB_ = 2
        xin = [A.alloc([TB + 4], F32) for _ in range(NB_)]
        acc = [A.alloc([TB], F32) for _ in range(NB_)]
        sg = [A.alloc([TB], F32) for _ in range(NB_)]
        ob = [A.alloc([TB], BF16) for _ in range(NB_)]
        tmo = [A.alloc([4, 128], BF16) for _ in range(NB_)]
        it = 0
        for (o, S) in self.seqs:
            for c in range(8):
                for t0 in range(0, S, TB):
                    tb = min(TB, S - t0)
                    b = it % NB_; it += 1
                    lo = 0 if t0 > 0 else 2
                    hi = tb + 4 if t0 + tb < S else tb + 2
                    if lo or hi < tb + 4:
                        em.op("pool", (lambda e, ap=xin[b][:, 0:tb + 4]: e.memset(ap, 0.0)), (), [xin[b]])
                    self.ld(xin[b], xin[b][:, lo:hi], dp["xbcT"][c * 128:(c + 1) * 128, o + t0 + lo:o + t0 + hi])
                    self.ts(acc[b][:, 0:tb], xin[b][:, 0:tb], cw[:, 0, c:c + 1], cb[:, c:c + 1], ALU.mult, ALU.add, [xin[b], cw, cb], [acc[b]])
                    for k in range(1, 5):
                        self.stt(acc[b][:, 0:tb], xin[b][:, k:k + tb], cw[:, k, c:c + 1], acc[b][:, 0:tb], ALU.mult, ALU.add,
                                 [xin[b], cw, acc[b]], [acc[b]])
                    self.act(sg[b][:, 0:tb], acc[b][:, 0:tb], AF.Exp, [acc[b]], [sg[b]], scale=-1.0)
                    self.ts(sg[b][:, 0:tb], sg[b][:, 0:tb], 1.0, None, ALU.add, None, [sg[b]], [sg[b]])
                    em.op("dve", (lambda e, o_=sg[b][:, 0:tb]: e.reciprocal(out=o_, in_=o_)), [sg[b]], [sg[b]])
                    self.tt(ob[b][:, 0:tb], acc[b][:, 0:tb], sg[b][:, 0:tb], ALU.mult, [acc[b], sg[b]], [ob[b]])
                    if c >= 4:
                        dst = dp["BT"] if c < 6 else dp["CT"]
                        rr = (c - 4) % 2
                        self.stq(ob[b], dst[rr * 128:(rr + 1) * 128, o + t0:o + t0 + tb], ob[b][:, 0:tb])
                    if c < 6:
                        nsub = tb // 128
                        pv = ps[it % 2].ap.bitcast(BF16)
                        for j in range(nsub):
                            self.tr(pv[:, j * 128:(j + 1) * 128], ob[b][:, j * 128:(j + 1) * 128], [ob[b]], [ps[it % 2]])
                        self.cp(tmo[b][:, 0:nsub, :], pv[:, 0:nsub * 128].rearrange("p (j c) -> p j c", c=128), [ps[it % 2]], [tmo[b]], eng="act")
                        if c < 4:
                            dst = dp["xtm"][o + t0:o + t0 + tb, c * 128:(c + 1) * 128]
                        else:
                            dst = dp["btm"][o + t0:o + t0 + tb, (c - 4) * 128:(c - 3) * 128]
                        self.stq(tmo[b], dst.rearrange("(j p) c -> p j c", p=128), tmo[b][:, 0:nsub, :])
        em.barrier()

    def stage_mla_prep(self):
        em, A, dp, ps = self.em, self.A, self.dp, self.ps
        A.reset()
        w_uq = self.load_w(A, dp["ev_w_uq"][0], 256, 768)
        w_ukv = self.load_w(A, dp["ev_w_ukv"][0], 128, 1024)
        qn = self.load_vec_fm(A, dp["ev_q_norm"][0], 256)
        kn = self.load_vec_fm(A, dp["ev_kv_norm"][0], 128)
        kmax = A.alloc([8], F32)
        em.op("dve", lambda e: e.memset(kmax[:, :], 0.0), (), [kmax])
        kmb = A.alloc([1], F32)
        cin = A.alloc([256], F32); kin = A.alloc([160], F32); rope = A.alloc([32], F32)
        junk = A.alloc([256], F32); tb_ = A.alloc([256], BF16)
        ss = A.alloc([2], F32); rs = A.alloc([2], F32)
        cT = A.alloc([2, 128], BF16)
        qf = A.alloc([8, 128], F32); qb = A.alloc([8, 128], BF16); sq = A.alloc([8, 96], F32); n2 = A.alloc([8], F32)
        em.op("dve", lambda e: e.memset(qf[:, :, :], 0.0), (), [qf])
        r1 = A.alloc([8, 16], F32); r2 = A.alloc([8, 16], F32)
        vb = A.alloc([8, 128], BF16)
        em.op("dve", lambda e: e.memset(vb[:, :, :], 0.0), (), [vb])
        oT = [A.alloc([128], BF16) for _ in range(4)]
        nt = self.ST // 128

        def rope_apply(dst, src, heads, cosb, sinb, reads):
            x1, x2 = src[:, :, 0:16], src[:, :, 16:32]
            self.tt(r1[:, 0:heads, :], x1, cosb, ALU.mult, reads, [r1])
            self.tt(r2[:, 0:heads, :], x2, sinb, ALU.mult, reads, [r2])
            self.tt(dst[:, :, 0:16], r1[:, 0:heads, :], r2[:, 0:heads, :], ALU.subtract, [r1, r2], [qf])
            self.tt(r1[:, 0:heads, :], x2, cosb, ALU.mult, reads, [r1])
            self.tt(r2[:, 0:heads, :], x1, sinb, ALU.mult, reads, [r2])
            self.tt(dst[:, :, 16:32], r1[:, 0:heads, :], r2[:, 0:heads, :], ALU.add, [r1, r2], [qf])

        for phase in getattr(self, "mla_phases", (0, 1)):
            if phase == 1:
                em.op("dve", lambda e: e.memset(qf[:, :, 96:128], 0.0), (), [qf])
                em.op("dve", lambda e: e.reduce_max(out=ss[:, 0:1], in_=kmax[:, :], axis=AX.X), [kmax], [ss])
                self.mm(ps[0][0:1, 0:128], ss[:, 0:1], self.idf[:, :], True, True, [ss, self.c_idf], [ps[0]])
                em.op("dve", lambda e: e.reduce_max(out=rs[0:1, 0:1], in_=ps[0][0:1, 0:128], axis=AX.X), [ps[0]], [rs])
                self.mm(ps[1][:, 0:1], self.c_ones[0:1, :], rs[0:1, 0:1], True, True, [rs, self.c_ones], [ps[1]])
                self.act(kmb[:, :], ps[1][:, 0:1], AF.Ln, [ps[1]], [kmb])
                self.act(kmb[:, :], kmb[:, :], AF.Exp, [kmb], [kmb], scale=0.5)
                self.ts(kmb[:, :], kmb[:, :], -1.0, None, ALU.mult, None, [kmb], [kmb])
            for i in range(nt):
                r0 = i * 128
                seq = 0 if r0 < self.S0 else 1
                pos0 = r0 - self.seqs[seq][0]
                self.ld(rope, rope[:, :], dp["c_rope"][pos0:pos0 + 128, :])
                if phase == 0:
                    self.ld(kin, kin[:, :], dp["ckv"][r0:r0 + 128, :])
                    pv = self.norm_T(A, kin[:, 0:128], [kin], 128, kn, None, ps[0], tb_, ss, rs, junk)
                    self.ts(cT[:, 0, :], pv[:, 0:128], kn[:, 0:1], None, ALU.mult, None, [ps[0], kn], [cT])
                    if getattr(self, "dbg_cut", 9) <= 1:
                        continue
                    for hh in range(2):
                        self.mm(ps[1 + hh][:, :], cT[:, 0, :], w_ukv[:, 0, hh * 512:(hh + 1) * 512], True, True, [cT, w_ukv], [ps[1 + hh]])
                    for hh in range(2):
                        kvv = ps[1 + hh][:, :].rearrange("p (h c) -> p h c", c=128)
                        self.cp(qf[:, hh * 4:(hh + 1) * 4, 0:64], kvv[:, :, 0:64], [ps[1 + hh]], [qf])
                        self.cp(vb[:, hh * 4:(hh + 1) * 4, 0:64], kvv[:, :, 64:128], [ps[1 + hh]], [vb])
                    if getattr(self, "dbg_cut", 9) <= 2:
                        continue
                    em.op("dve", lambda e: e.memset(vb[:, :, 64:128], 1.0), (), [vb])
                    em.op("dve", lambda e: e.memset(qf[:, :, 96:128], 1.0), (), [qf])
                    kx1, kx2, cs_, sn_ = kin[:, 128:144], kin[:, 144:160], rope[:, 0:16], rope[:, 16:32]
                    self.tt(r1[:, 0, :], kx1, cs_, ALU.mult, [kin, rope], [r1])
                    self.tt(r2[:, 0, :], kx2, sn_, ALU.mult, [kin, rope], [r2])
                    self.tt(qf[:, 0, 64:80], r1[:, 0, :], r2[:, 0, :], ALU.subtract, [r1, r2], [qf])
                    self.tt(r1[:, 0, :], kx2, cs_, ALU.mult, [kin, rope], [r1])
                    self.tt(r2[:, 0, :], kx1, sn_, ALU.mult, [kin, rope], [r2])
                    self.tt(qf[:, 0, 80:96], r1[:, 0, :], r2[:, 0, :], ALU.add, [r1, r2], [qf])
                    for h in range(1, 8):
                        self.cp(qf[:, h, 64:96], qf[:, 0, 64:96], [qf], [qf])
                    if getattr(self, "dbg_cut", 9) <= 3:
                        continue
                    self.stq(vb, dp["V"][r0:r0 + 128, :, :], vb[:, :, :])
                    self.tt(sq[:, :, :], qf[:, :, 0:96], qf[:, :, 0:96], ALU.mult, [qf], [sq])
                    em.op("dve", lambda e: e.reduce_sum(out=n2[:, :], in_=sq[:, :, :], axis=AX.X), [sq], [n2])
                    self.tt(kmax[:, :], kmax[:, :], n2[:, :], ALU.max, [kmax, n2], [kmax])
                    dstT = dp["KT"]
                else:
                    self.ld(cin, cin[:, :], dp["cq"][r0:r0 + 128, :])
                    pv = self.norm_T(A, cin[:, :], [cin], 256, qn, None, ps[0], tb_, ss, rs, junk)
                    self.tt(cT[:, :, :], pv[:, 0:256].rearrange("p (k t) -> p k t", t=128), bc(qn[:, :], [128, 2, 128], 2), ALU.mult, [ps[0], qn], [cT])
                    for (c0, c1, p) in ((0, 480, ps[1]), (480, 768, ps[2])):
                        for k in range(2):
                            self.mm(p[:, 0:c1 - c0], cT[:, k, :], w_uq[:, k, c0:c1], k == 0, k == 1, [cT, w_uq], [p])
                    self.cp(qf[:, 0:5, 0:96], ps[1][:, 0:480].rearrange("p (h c) -> p h c", c=96), [ps[1]], [qf])
                    self.cp(qf[:, 5:8, 0:96], ps[2][:, 0:288].rearrange("p (h c) -> p h c", c=96), [ps[2]], [qf], eng="act")
                    self.tt(sq[:, :, :], qf[:, :, 0:96], qf[:, :, 0:96], ALU.mult, [qf], [sq])
                    em.op("dve", lambda e: e.reduce_sum(out=n2[:, :], in_=sq[:, :, :], axis=AX.X), [sq], [n2])
                    self.act(n2[:, :], n2[:, :], AF.Ln, [n2, self.c_eps], [n2], bias=self.epsb[:, 0:1])
                    self.act(n2[:, :], n2[:, :], AF.Exp, [n2], [n2], scale=0.5)
                    self.ts(qf[:, :, 96], n2[:, :], kmb[:, 0:1], None, ALU.mult, None, [n2, kmb], [qf])
                    self.cp(sq[:, :, 0:32], qf[:, :, 64:96], [qf], [sq])
                    rope_apply(qf[:, :, 64:96], sq[:, :, 0:32], 8, bc(rope[:, 0:16], [128, 8, 16], 1), bc(rope[:, 16:32], [128, 8, 16], 1), [sq, rope])
                    dstT = dp["QT"]
                if getattr(self, "dbg_cut", 9) <= 4:
                    continue
                self.cp(qb[:, :, :], qf[:, :, :], [qf], [qb], eng="act")
                if getattr(self, "dbg_cut", 9) <= 5:
                    continue
                for h in range(8):
                    p = ps[3 + (h % 4)]
                    pvv = p.ap.bitcast(BF16)
                    self.tr(pvv[:, 0:128], qb[:, h, :], [qb], [p])
                    o_ = oT[h % 4]
                    self.cp(o_[:, :], pvv[:, 0:128], [p], [o_], eng="act" if h % 2 else "dve")
                    self.stq(o_, dstT[h, :, r0:r0 + 128], o_[:, :])
        em.barrier()

    def stage_ssd1(self):
        em, A, dp, ps = self.em, self.A, self.dp, self.ps
        A.reset()
        tri = self.c_tri
        NB_ = 2
        bt = [A.alloc([256], BF16) for _ in range(NB_)]
        xt = [A.alloc([512], BF16) for _ in range(NB_)]
        dt = [A.alloc([32], F32) for _ in range(NB_)]
        H = [A.alloc([512], F32) for _ in range(2)]
        Hb = [A.alloc([512], BF16) for _ in range(4)]
        cd = A.alloc([16], F32); dst = A.alloc([16], F32); coef = A.alloc([8], F32)
        xs = A.alloc([512], BF16); tmp = A.alloc([512], F32)
        it = 0
        for (o, S) in self.seqs:
            ncnk = S // 128
            for d in (0, 1):
                em.op("dve", (lambda e, ap=H[d][:, :]: e.memset(ap, 0.0)), (), [H[d]])
            for step in range(ncnk):
                for d in (0, 1):
                    c = step if d == 0 else ncnk - 1 - step
                    b = it % NB_; hb = Hb[it % 4]; it += 1
                    r0 = o + c * 128
                    gc = r0 // 128
                    self.ld(bt[b], bt[b][:, :], dp["btm"][r0:r0 + 128, :])
                    self.ld(xt[b], xt[b][:, :], dp["xtm"][r0:r0 + 128, :])
                    self.ld(dt[b], dt[b][:, :], dp["dtp"][r0:r0 + 128, :])
                    adt = dt[b][:, 16 + 8 * d:24 + 8 * d]
                    self.mm(ps[0][:, 0:8], self.c_ones[:, :], adt, True, True, [self.c_ones, dt[b]], [ps[0]])
                    self.mm(ps[0][:, 8:16], tri[:, 2 + d, :], adt, True, True, [tri, dt[b]], [ps[0]])
                    self.act(cd[:, 0:16], ps[0][:, 0:16], AF.Exp, [ps[0]], [cd])
                    self.tt(coef[:, :], cd[:, 8:16], dt[b][:, 8 * d:8 * d + 8], ALU.mult, [cd, dt[b]], [coef])
                    self.tt(xs[:, :].rearrange("p (h c) -> p h c", c=64), xt[b][:, :].rearrange("p (h c) -> p h c", c=64),
                            bc(coef[:, :], [128, 8, 64], 2), ALU.mult, [xt[b], coef], [xs])
                    for g in range(2):
                        self.mm(ps[1][:, g * 256:(g + 1) * 256], bt[b][:, g * 128:(g + 1) * 128], xs[:, g * 256:(g + 1) * 256], True, True,
                                [bt[b], xs], [ps[1]])
                    self.cp(hb[:, :], H[d][:, :], [H[d]], [hb], eng="act")
                    self.stq(hb, dp["Hst"][d, gc], hb[:, :])
                    self.tt(tmp[:, :].rearrange("p (h c) -> p h c", c=64), H[d][:, :].rearrange("p (h c) -> p h c", c=64),
                            bc(cd[:, 0:8], [128, 8, 64], 2), ALU.mult, [H[d], cd], [tmp])
                    self.tt(H[d][:, :], tmp[:, :], ps[1][:, :], ALU.add, [tmp, ps[1]], [H[d]])
        em.barrier()

    def stage_ssd2(self):
        em, A, dp, ps = self.em, self.A, self.dp, self.ps
        A.reset()
        tri = self.c_tri
        dsk = A.alloc([8], F32)
        self.ld(dsk, dsk[:, :], dp["ev_d_skip"][0].partition_broadcast(128))
        snw = self.load_vec_bc(A, dp["ev_ssm_norm"][0], 512)
        NB_ = 2
        ct = [A.alloc([2, 128], BF16) for _ in range(NB_)]
        btT = [A.alloc([2, 128], BF16) for _ in range(NB_)]
        xt = [A.alloc([512], BF16) for _ in range(NB_)]
        zt = [A.alloc([512], F32) for _ in range(NB_)]
        dt = [A.alloc([32], F32) for _ in range(NB_)]
        hh = [A.alloc([2, 512], BF16) for _ in range(NB_)]
        cbm = A.alloc([4, 128], F32)
        rhs = A.alloc([4, 128], F32); dec = A.alloc([4, 128], F32)
        mT = A.alloc([16, 128], BF16)
        xdt = A.alloc([2, 512], BF16)
        ee = A.alloc([16], F32)
        y = A.alloc([512], F32); t1 = A.alloc([512], F32); sg = A.alloc([512], F32); junk = A.alloc([512], F32)
        ss = A.alloc([2], F32); rs = A.alloc([2], F32)
        yb = A.alloc([512], BF16); yT = [A.alloc([4, 128], BF16) for _ in range(2)]
        v3 = lambda ap: ap.rearrange("p (h c) -> p h c", c=64)
        nt = self.ST // 128
        for i in range(nt):
            b = i % NB_
            r0 = i * 128
            self.ld(ct[b], ct[b][:, :, :], dp["CT"][:, r0:r0 + 128].rearrange("(g p) t -> p g t", p=128))
            self.ld(btT[b], btT[b][:, :, :], dp["BT"][:, r0:r0 + 128].rearrange("(g p) t -> p g t", p=128))
            self.ld(xt[b], xt[b][:, :], dp["xtm"][r0:r0 + 128, :])
            self.ld(zt[b], zt[b][:, :], dp["z"][r0:r0 + 128, :])
            self.ld(dt[b], dt[b][:, :], dp["dtp"][r0:r0 + 128, :])
            self.ld(hh[b], hh[b][:, :, :], dp["Hst"][:, i].rearrange("d p c -> p d c"))
            for g in range(2):
                self.mm(ps[0][:, g * 128:(g + 1) * 128], btT[b][:, g, :], ct[b][:, g, :], True, True, [btT[b], ct[b]], [ps[0]])
            for d in range(2):
                self.tt(cbm[:, 2 * d:2 * d + 2, :], ps[0][:, 0:256].rearrange("p (g l) -> p g l", l=128), bc(tri[:, d, :], [128, 2, 128], 1),
                        ALU.mult, [ps[0], tri], [cbm])
            for d in range(2):
                self.mm(ps[1][:, d * 8:(d + 1) * 8], tri[:, d, :], dt[b][:, 16 + 8 * d:24 + 8 * d], True, True, [tri, dt[b]], [ps[1]])
            self.act(ee[:, :], ps[1][:, 0:16], AF.Exp, [ps[1]], [ee])
            for d in range(2):
                for g in range(2):
                    adt = dt[b][:, 16 + 8 * d + 4 * g:16 + 8 * d + 4 * g + 4]
                    self.tt(rhs[:, :, :], bc(tri[:, d, :], [128, 4, 128], 1), bc(adt, [128, 4, 128], 2), ALU.mult, [tri, dt[b]], [rhs])
                    p = ps[2 + (2 * d + g) % 2]
                    self.mm(p[:, :], tri[:, 2 + d, :], rhs[:, :, :].rearrange("p h l -> p (h l)"), True, True, [tri, rhs], [p])
                    self.act(dec[:, :, :].rearrange("p h l -> p (h l)"), p[:, :], AF.Exp, [p], [dec])
                    k0 = 8 * d + 4 * g
                    self.tt(mT[:, k0:k0 + 4, :], dec[:, :, :], bc(cbm[:, 2 * d + g, :], [128, 4, 128], 1), ALU.mult, [dec, cbm], [mT])
                self.tt(v3(xdt[:, d, :]), v3(xt[b][:, :]), bc(dt[b][:, 8 * d:8 * d + 8], [128, 8, 64], 2), ALU.mult, [xt[b], dt[b]], [xdt])
            for h in range(8):
                for d in range(2):
                    self.mm(ps[4][:, h * 64:(h + 1) * 64], mT[:, 8 * d + h, :], xdt[:, d, h * 64:(h + 1) * 64], d == 0, d == 1, [mT, xdt], [ps[4]])
            for d in range(2):
                for g in range(2):
                    self.mm(ps[5 + d][:, g * 256:(g + 1) * 256], ct[b][:, g, :], hh[b][:, d, g * 256:(g + 1) * 256], True, True, [ct[b], hh[b]], [ps[5 + d]])
            self.tt(v3(y[:, :]), v3(ps[5][:, :]), bc(ee[:, 0:8], [128, 8, 64], 2), ALU.mult, [ps[5], ee], [y])
            self.tt(v3(t1[:, :]), v3(ps[6][:, :]), bc(ee[:, 8:16], [128, 8, 64], 2), ALU.mult, [ps[6], ee], [t1])
            self.tt(y[:, :], y[:, :], t1[:, :], ALU.add, [y, t1], [y])
            self.tt(y[:, :], y[:, :], ps[4][:, :], ALU.add, [y, ps[4]], [y])
            self.tt(v3(t1[:, :]), v3(xt[b][:, :]), bc(dsk[:, :], [128, 8, 64], 2), ALU.mult, [xt[b], dsk], [t1])
            self.tt(y[:, :], y[:, :], t1[:, :], ALU.add, [y, t1], [y])
            self.act(sg[:, :], zt[b][:, :], AF.Exp, [zt[b]], [sg], scale=-1.0)
            self.ts(sg[:, :], sg[:, :], 1.0, None, ALU.add, None, [sg], [sg])
            em.op("dve", lambda e: e.reciprocal(out=sg[:, :], in_=sg[:, :]), [sg], [sg])
            self.tt(sg[:, :], sg[:, :], zt[b][:, :], ALU.mult, [sg, zt[b]], [sg])
            self.tt(y[:, :], y[:, :], sg[:, :], ALU.mult, [y, sg], [y])
            for g in range(2):
                self.act(junk[:, 0:256], y[:, g * 256:(g + 1) * 256], AF.Square, [y], [junk, ss], accum=ss[:, g:g + 1])
            self.rstd_from_ss(ss, rs, 2, 256)
            self.tt(y[:, :].rearrange("p (g c) -> p g c", c=256), y[:, :].rearrange("p (g c) -> p g c", c=256), bc(rs[:, 0:2], [128, 2, 256], 2),
                    ALU.mult, [y, rs], [y])
            self.tt(yb[:, :], y[:, :], snw[:, :], ALU.mult, [y, snw], [yb])
            pv = ps[7].ap.bitcast(BF16)
            for k in range(4):
                self.tr(pv[:, k * 128:(k + 1) * 128], yb[:, k * 128:(k + 1) * 128], [yb], [ps[7]])
            yo = yT[i % 2]
            self.cp(yo[:, :, :], pv[:, 0:512].rearrange("p (k t) -> p k t", t=128), [ps[7]], [yo], eng="act")
            self.stq(yo, dp["mixT"][0:512, r0:r0 + 128].rearrange("(k p) t -> p k t", p=128), yo[:, :, :])
        em.barrier()

    def stage_attn(self):
        em, A, dp, ps = self.em, self.A, self.dp, self.ps
        A.reset()
        scale = 96.0 ** -0.5
        for (o, S) in self.seqs:
            A.reset()
            nkt = S // 128
            QB = min(512, S)
            KT = [A.alloc([S], BF16) for _ in range(2)]
            VV = [A.alloc([nkt, 128], BF16) for _ in range(2)]
            qt = [A.alloc([QB], BF16) for _ in range(2)]
            pT = [A.alloc([2, QB], BF16) for _ in range(3)]
            osb = A.alloc([QB], F32); rc = A.alloc([QB], F32); ob = [A.alloc([QB], BF16) for _ in range(2)]
            it = 0
            for h in range(8):
                kt, vv = KT[h % 2], VV[h % 2]
                self.ld(kt, kt[:, :], dp["KT"][h, :, o:o + S])
                for k0 in range(0, nkt, 16):
                    k1 = min(nkt, k0 + 16)
                    self.ld(vv, vv[:, k0:k1, :], dp["V"][o + k0 * 128:o + k1 * 128, h, :].rearrange("(k p) c -> p k c", p=128))
                for q0 in range(0, S, QB):
                    qq = qt[(it // 1) % 2]
                    self.ld(qq, qq[:, :], dp["QT"][h, :, o + q0:o + q0 + QB])
                    po = ps[6 + it % 2]
                    for k in range(0, nkt, 2):
                        j = k % 4
                        pa, pb = ps[j], ps[j + 1]
                        self.mm(pa[:, 0:QB], kt[:, k * 128:(k + 1) * 128], qq[:, :], True, True, [kt, qq], [pa])
                        self.mm(pb[:, 0:QB], kt[:, (k + 1) * 128:(k + 2) * 128], qq[:, :], True, True, [kt, qq], [pb])
                        pt = pT[(k // 2) % 3]
                        if QB == 512:
                            self.act(pt[:, :, :].rearrange("p a b -> p (a b)"), self.psall[:, j:j + 2, :].rearrange("p a b -> p (a b)"), AF.Exp, [pa, pb], [pt], scale=scale)
                        else:
                            self.act(pt[:, 0, :], pa[:, 0:QB], AF.Exp, [pa], [pt], scale=scale)
                            self.act(pt[:, 1, :], pb[:, 0:QB], AF.Exp, [pb], [pt], scale=scale)
                        self.mm(po[:, 0:QB], vv[:, k, :], pt[:, 0, :], k == 0, False, [vv, pt], [po])
                        self.mm(po[:, 0:QB], vv[:, k + 1, :], pt[:, 1, :], False, k + 2 >= nkt, [vv, pt], [po])
                    self.cp(osb[:, :], po[:, 0:QB], [po], [osb])
                    em.op("dve", (lambda e, o_=rc[64:65, :], i_=osb[64:65, :]: e.reciprocal(out=o_, in_=i_)), [osb], [rc])
                    self.mm(ps[4 + it % 2][0:64, 0:QB], self.c_ones[64:65, 0:64], rc[64:65, :], True, True, [self.c_ones, rc], [ps[4 + it % 2]])
                    obb = ob[it % 2]
                    self.tt(obb[0:64, :], osb[0:64, :], ps[4 + it % 2][0:64, 0:QB], ALU.mult, [osb, ps[4 + it % 2]], [obb])
                    self.stq(obb, dp["mixT"][512 + h * 64:512 + (h + 1) * 64, o + q0:o + q0 + QB], obb[0:64, :])
                    it += 1
            em.barrier()

    def stage_mixout(self, L):
        em, A, dp, ps = self.em, self.A, self.dp, self.ps
        A.reset()
        wmix = self.load_w(A, dp["ev_w_out"][0] if L == 0 else dp["od_w_mix"][0], D, D)
        wq = self.load_w(A, dp["xa_wq"][L], D, D)
        wo = self.load_w(A, dp["xa_wo"][L], D, D)
        npost = self.load_vec_bc(A, dp["norm_mix_post"][L], D)
        nxpre = self.load_vec_fm(A, dp["norm_xa_pre"][L], D)
        nxpost = self.load_vec_bc(A, dp["norm_xa_post"][L], D)
        nmem = self.load_vec_fm(A, dp["norm_mem"][L], D)
        kT = [A.alloc([8, NMEM], BF16) for _ in range(2)]
        vm = [A.alloc([2, D], BF16) for _ in range(2)]
        kmx = [A.alloc([4], F32) for _ in range(2)]
        mt = A.alloc([D], F32); mb = A.alloc([D], BF16); junk = A.alloc([D], F32); mT = A.alloc([8, 128], BF16)
        ss = A.alloc([8], F32); rs = A.alloc([8], F32)
        kk = A.alloc([D], F32); k2 = A.alloc([4], F32); km = A.alloc([4], F32)
        wkv = A.alloc([8, 512], BF16)
        for s in range(2):
            em.op("dve", (lambda e, ap=km[:, :]: e.memset(ap, 0.0)), (), [km])
            for mc in range(2):
                self.ld(mt, mt[:, :], dp["mem"][s, mc * 128:(mc + 1) * 128, :])
                pv = self.norm_T(A, mt[:, :], [mt], D, nmem, None, ps[0], mb, ss, rs, junk)
                self.tt(mT[:, :, :], pv[:, 0:1024].rearrange("p (k t) -> p k t", t=128), bc(nmem[:, :], [128, 8, 128], 2), ALU.mult, [ps[0], nmem], [mT])
                for nq in range(4):
                    for k in range(8):
                        self.em.dma("pool", wkv[:, k, :], dp["xa_wkv"][L].rearrange("(k p) n -> p k n", p=128)[:, k, nq * 512:(nq + 1) * 512], writes=[wkv])
                    for k in range(8):
                        self.mm(ps[1][:, :], mT[:, k, :], wkv[:, k, :], k == 0, k == 7, [mT, wkv], [ps[1]])
                    if nq < 2:
                        self.cp(kk[:, nq * 512:(nq + 1) * 512], ps[1][:, :], [ps[1]], [kk])
                    else:
                        self.cp(vm[s][:, mc, (nq - 2) * 512:(nq - 1) * 512], ps[1][:, :], [ps[1]], [vm[s]], eng="act")
                self.tt(junk[:, :], kk[:, :], kk[:, :], ALU.mult, [kk], [junk])
                em.op("dve", lambda e: e.reduce_sum(out=k2[:, :], in_=junk[:, :].rearrange("p (h c) -> p h c", c=256), axis=AX.X), [junk], [k2])
                self.tt(km[:, :], km[:, :], k2[:, :], ALU.max, [km, k2], [km])
                self.cp(mb[:, :], kk[:, :], [kk], [mb], eng="act")
                pv = ps[2].ap.bitcast(BF16)
                for k in range(8):
                    self.tr(pv[:, k * 128:(k + 1) * 128], mb[:, k * 128:(k + 1) * 128], [mb], [ps[2]])
                self.cp(kT[s][:, :, mc * 128:(mc + 1) * 128], pv[:, 0:1024].rearrange("p (k t) -> p k t", t=128), [ps[2]], [kT[s]])
            self.mm(ps[3][0:4, 0:128], km[:, :], self.idf[:, :], True, True, [km, self.c_idf], [ps[3]])
            em.op("dve", lambda e: e.reduce_max(out=rs[0:4, 0:1], in_=ps[3][0:4, 0:128], axis=AX.X), [ps[3]], [rs])
            self.ts(ss[0:4, 0:4], self.idf[0:4, 0:4], rs[0:4, 0:1], None, ALU.mult, None, [rs, self.c_idf], [ss])
            self.mm(ps[3][:, 128:132], self.c_ones[0:4, :], ss[0:4, 0:4], True, True, [self.c_ones, ss], [ps[3]])
            self.act(kmx[s][:, :], ps[3][:, 128:132], AF.Ln, [ps[3]], [kmx[s]])
            self.act(kmx[s][:, :], kmx[s][:, :], AF.Exp, [kmx[s]], [kmx[s]], scale=0.5)
            self.ts(kmx[s][:, :], kmx[s][:, :], -1.0, None, ALU.mult, None, [kmx[s]], [kmx[s]])
        NB_ = 2
        NTOK = 512 if min(self.S0, self.S1) % 512 == 0 else 256
        ht = [A.alloc([NTOK // 128, D], F32) for _ in range(NB_)]
        mi = [A.alloc([D], BF16) for _ in range(NB_)]
        miT = [A.alloc([8, NTOK], BF16) for _ in range(NB_)]
        hb = A.alloc([D], BF16)
        hT = A.alloc([8, NTOK], BF16)
        qT = A.alloc([8, NTOK], BF16); q2 = A.alloc([NTOK], BF16)
        shf = A.alloc([NTOK], F32)
        pT = A.alloc([2, NTOK], BF16)
        den = A.alloc([NTOK], F32)
        oT = A.alloc([8, NTOK], BF16)
        tmp = A.alloc([D], F32)
        xs = 256.0 ** -0.5
        for (si, (o, S)) in enumerate(self.seqs):
            for t0 in range(0, S, NTOK):
                b = (t0 // NTOK) % NB_
                r0 = o + t0
                nsub = NTOK // 128
                hsrc = dp["x"] if L == 0 else dp["h"]
                self.ld(ht[b], ht[b][:, :, :], hsrc[r0:r0 + NTOK, :].rearrange("(j p) c -> p j c", p=128))
                if L == 0:
                    self.ld(miT[b], miT[b][:, :, :], dp["mixT"][:, r0:r0 + NTOK].rearrange("(k p) t -> p k t", p=128))
                else:
                    for j in range(nsub):
                        self.ld(mi[b], mi[b][:, :], dp["f"][r0 + j * 128:r0 + (j + 1) * 128, :])
                        pv = ps[0].ap.bitcast(BF16)
                        for k in range(8):
                            self.tr(pv[:, k * 128:(k + 1) * 128], mi[b][:, k * 128:(k + 1) * 128], [mi[b]], [ps[0]])
                        self.cp(miT[b][:, :, j * 128:(j + 1) * 128], pv[:, 0:1024].rearrange("p (k t) -> p k t", t=128), [ps[0]], [miT[b]])
                for j in range(nsub):
                    hj = ht[b][:, j, :]
                    for hf in range(2):
                        for k in range(8):
                            self.mm(ps[1 + hf][:, :], miT[b][:, k, j * 128:(j + 1) * 128], wmix[:, k, hf * 512:(hf + 1) * 512], k == 0, k == 7, [miT[b], wmix], [ps[1 + hf]])
                    self.post_norm_add(hj, ht[b], [ps[1], ps[2]], npost, ss, rs, junk, tmp)
                    pv = self.norm_T(A, hj, [ht[b]], D, nxpre, None, ps[0], hb, ss, rs, junk)
                    self.tt(hT[:, :, j * 128:(j + 1) * 128], pv[:, 0:1024].rearrange("p (k t) -> p k t", t=128), bc(nxpre[:, :], [128, 8, 128], 2), ALU.mult, [ps[0], nxpre], [hT])
                for c in range(8):
                    p = ps[3 + c % 2]
                    for k in range(8):
                        self.mm(p[:, 0:NTOK], wq[:, k, c * 128:(c + 1) * 128], hT[:, k, :], k == 0, k == 7, [hT, wq], [p])
                    self.cp(qT[:, c, :], p[:, 0:NTOK], [p], [qT], eng="act" if c % 2 else "dve")
                for h in range(4):
                    for cc in range(2):
                        self.tt(q2[:, :], qT[:, 2 * h + cc, :], qT[:, 2 * h + cc, :], ALU.mult, [qT], [q2])
                        self.mm(ps[5][:, 0:NTOK], self.c_onesb[:, :], q2[:, :], cc == 0, cc == 1, [self.c_onesb, q2], [ps[5]])
                    self.act(shf[:, :], ps[5][:, 0:NTOK], AF.Ln, [ps[5], self.c_eps], [shf], bias=self.epsb[:, 0:1])
                    self.act(shf[:, :], shf[:, :], AF.Exp, [shf], [shf], scale=0.5)
                    self.ts(shf[:, :], shf[:, :], kmx[si][:, h:h + 1], xs, ALU.mult, ALU.mult, [shf, kmx[si]], [shf])
                    for mc in range(2):
                        p = ps[6]
                        for cc in range(2):
                            self.mm(p[:, 0:NTOK], kT[si][:, 2 * h + cc, mc * 128:(mc + 1) * 128], qT[:, 2 * h + cc, :], cc == 0, cc == 1, [kT[si], qT], [p])
                        self.stt(den[:, :], p[:, 0:NTOK], xs, shf[:, :], ALU.mult, ALU.add, [p, shf], [den])
                        self.act(pT[:, mc, :], den[:, :], AF.Exp, [den], [pT])
                    for mc in range(2):
                        self.mm(ps[7][:, 0:NTOK], self.c_onesb[:, :], pT[:, mc, :], mc == 0, mc == 1, [self.c_onesb, pT], [ps[7]])
                    em.op("dve", lambda e: e.reciprocal(out=den[:, :], in_=ps[7][:, 0:NTOK]), [ps[7]], [den])
                    for cc in range(2):
                        p = ps[3 + cc]
                        for mc in range(2):
                            self.mm(p[:, 0:NTOK], vm[si][:, mc, (2 * h + cc) * 128:(2 * h + cc + 1) * 128], pT[:, mc, :], mc == 0, mc == 1, [vm[si], pT], [p])
                        self.tt(oT[:, 2 * h + cc, :], p[:, 0:NTOK], den[:, :], ALU.mult, [p, den], [oT])
                for j in range(nsub):
                    hj = ht[b][:, j, :]
                    for hf in range(2):
                        for k in range(8):
                            self.mm(ps[1 + hf][:, :], oT[:, k, j * 128:(j + 1) * 128], wo[:, k, hf * 512:(hf + 1) * 512], k == 0, k == 7, [oT, wo], [ps[1 + hf]])
                    self.post_norm_add(hj, ht[b], [ps[1], ps[2]], nxpost, ss, rs, junk, tmp)
                self.stq(ht[b], dp["h"][r0:r0 + NTOK, :].rearrange("(j p) c -> p j c", p=128), ht[b][:, :, :])
        em.barrier()

    def post_norm_add(self, hj, ht, pss, wbc, ss, rs, junk, tmp):
        for hf in range(2):
            self.act(junk[:, hf * 512:(hf + 1) * 512], pss[hf][:, :], AF.Square, [pss[hf]], [junk, ss], accum=ss[:, hf:hf + 1])
        self.tt(ss[:, 0:1], ss[:, 0:1], ss[:, 1:2], ALU.add, [ss], [ss])
        self.rstd_from_ss(ss, rs, 1, D)
        for hf in range(2):
            self.tt(tmp[:, hf * 512:(hf + 1) * 512], pss[hf][:, :], wbc[:, hf * 512:(hf + 1) * 512], ALU.mult, [pss[hf], wbc], [tmp])
        self.stt(hj, tmp[:, :], rs[:, 0:1], hj, ALU.mult, ALU.add, [tmp, rs, ht], [ht])

    def stage_ffn(self, L, yo0, yo1):
        em, A, dp, ps = self.em, self.A, self.dp, self.ps
        A.reset()
        wgu = self.load_w(A, dp["ffn_w_gu"][L], D, 2 * DFF)
        wd = self.load_w(A, dp["ffn_w_down"][L], DFF, D)
        npre = self.load_vec_fm(A, dp["norm_ffn_pre"][L], D)
        npost = self.load_vec_bc(A, dp["norm_ffn_post"][L], D)
        NTOK = 512 if min(self.S0, self.S1) % 512 == 0 else 256
        NB_ = 1
        ht = [A.alloc([NTOK // 128, D], F32) for _ in range(NB_)]
        hb = A.alloc([D], BF16); junk = A.alloc([D], BF16); tmp = A.alloc([D], F32)
        hT = A.alloc([8, NTOK], BF16)
        aT = A.alloc([22, NTOK], BF16)
        sg = [A.alloc([NTOK], F32) for _ in range(2)]
        ss = A.alloc([2], F32); rs = A.alloc([2], F32)
        it = 0
        for (si, (o, S)) in enumerate(self.seqs):
            for t0 in range(0, S, NTOK):
                b = it % NB_; it += 1
                r0 = o + t0
                nsub = NTOK // 128
                self.ld(ht[b], ht[b][:, :, :], dp["h"][r0:r0 + NTOK, :].rearrange("(j p) c -> p j c", p=128))
                for j in range(nsub):
                    pv = self.norm_T(A, ht[b][:, j, :], [ht[b]], D, npre, None, ps[0], hb, ss, rs, junk)
                    self.tt(hT[:, :, j * 128:(j + 1) * 128], pv[:, 0:1024].rearrange("p (k t) -> p k t", t=128), bc(npre[:, :], [128, 8, 128], 2), ALU.mult, [ps[0], npre], [hT])
                for f in range(22):
                    pg, pu = ps[1 + 2 * (f % 2)], ps[2 + 2 * (f % 2)]
                    for k in range(8):
                        self.mm(pg[:, 0:NTOK], wgu[:, k, f * 128:(f + 1) * 128], hT[:, k, :], k == 0, k == 7, [hT, wgu], [pg])
                    for k in range(8):
                        self.mm(pu[:, 0:NTOK], wgu[:, k, DFF + f * 128:DFF + (f + 1) * 128], hT[:, k, :], k == 0, k == 7, [hT, wgu], [pu])
                    s_ = sg[f % 2]
                    self.act(s_[:, :], pg[:, 0:NTOK], AF.Exp, [pg], [s_], scale=-1.0)
                    self.ts(s_[:, :], s_[:, :], 1.0, None, ALU.add, None, [s_], [s_])
                    em.op("dve", (lambda e, a_=s_[:, :]: e.reciprocal(out=a_, in_=a_)), [s_], [s_])
                    self.tt(s_[:, :], s_[:, :], pg[:, 0:NTOK], ALU.mult, [s_, pg], [s_])
                    self.tt(aT[:, f, :], s_[:, :], pu[:, 0:NTOK], ALU.mult, [s_, pu], [aT])
                for j in range(nsub):
                    for hf in range(2):
                        for k in range(22):
                            self.mm(ps[5 + hf][:, :], aT[:, k, j * 128:(j + 1) * 128], wd[:, k, hf * 512:(hf + 1) * 512], k == 0, k == 21, [aT, wd], [ps[5 + hf]])
                    self.post_norm_add(ht[b][:, j, :], ht[b], [ps[5], ps[6]], npost, ss, rs, junk, tmp)
                if L == 0:
                    self.stq(ht[b], dp["h"][r0:r0 + NTOK, :].rearrange("(j p) c -> p j c", p=128), ht[b][:, :, :])
                elif si == 0:
                    self.stq(ht[b], yo0[t0:t0 + NTOK, :].rearrange("(j p) c -> p j c", p=128), ht[b][:, :, :])
                else:
                    self.stq(ht[b], yo1[t0:t0 + NTOK, :].rearrange("(j p) c -> p j c", p=128), ht[b][:, :, :])
        em.barrier()

    def stage_F(self):
        em, A, dp, ps = self.em, self.A, self.dp, self.ps
        A.reset()
        cs = self.load_w(A, dp["c_cs"], 256, 512)
        npre = self.load_vec_fm(A, dp["norm_mix_pre"][1], D)
        NB_ = 2
        ht = [A.alloc([D], F32) for _ in range(NB_)]
        hb = A.alloc([D], BF16); junk = A.alloc([D], F32)
        hT = A.alloc([8, 128], BF16)
        ss = A.alloc([2], F32); rs = A.alloc([2], F32)
        zo = [A.alloc([2, D], BF16) for _ in range(NB_)]
        for i in range(self.ST // 128):
            b = i % NB_
            r0 = i * 128
            self.ld(ht[b], ht[b][:, :], dp["h"][r0:r0 + 128, :])
            pv = self.norm_T(A, ht[b][:, :], [ht[b]], D, npre, None, ps[0], hb, ss, rs, junk)
            self.tt(hT[:, :, :], pv[:, 0:1024].rearrange("p (k t) -> p k t", t=128), bc(npre[:, :], [128, 8, 128], 2), ALU.mult, [ps[0], npre], [hT])
            for g in range(4):
                p = ps[1 + g % 4]
                for k in range(2):
                    self.mm(p[:, :], hT[:, 2 * g + k, :], cs[:, k, :], k == 0, k == 1, [hT, cs], [p])
                self.cp(zo[b][:, :, g * 256:(g + 1) * 256], p[:, :].rearrange("p (r c) -> p r c", c=256), [p], [zo[b]], eng="act" if g % 2 else "dve")
            self.stq(zo[b], dp["Zr"][r0:r0 + 128, :], zo[b][:, 0, :])
            self.stq(zo[b], dp["Zi"][r0:r0 + 128, :], zo[b][:, 1, :])
        em.barrier()

    def stage_dft(self):
        em, A, dp, ps = self.em, self.A, self.dp, self.ps
        CB = 64
        for (si, (o, S)) in enumerate(self.seqs):
            A.reset()
            NB = S // 128
            d1 = A.alloc([2, 2 * NB], BF16)
            self.em.dma("pool", d1[0:NB, :, :], dp[f"c_d1_{si}"].rearrange("a b n -> b a n"), writes=[d1])
            E = A.alloc([NB, 2, 128], BF16)
            self.ld(E, E[:, :, :, :], dp[f"c_e_{si}"])
            zt = [A.alloc([2, 128, CB], BF16) for _ in range(2)]
            nbuf = 2 if NB <= 64 else 1
            Y = [A.alloc([2, NB, CB], BF16) for _ in range(nbuf)] * (2 // nbuf)
            fo = [A.alloc([NB, CB], BF16) for _ in range(nbuf)] * (2 // nbuf)
            cpb = min(CB, max(1, 512 // (2 * NB)))
            for cb in range(D // CB):
                b = cb % 2
                c0 = cb * CB
                for ri, nm in enumerate(["Zr", "Zi"]):
                    for b0 in range(0, NB, 16):
                        b1 = min(NB, b0 + 16)
                        self.ld(zt[b], zt[b][b0:b1, ri, :, :], dp[nm][o + b0 * 128:o + b1 * 128, c0:c0 + CB].rearrange("(b q) c -> b q c", q=128))
                for cg in range(0, CB, cpb):
                    p = ps[(cg // cpb) % 4]
                    for cc in range(cpb):
                        c = cg + cc
                        self.mm(p[:, cc * 2 * NB:(cc + 1) * 2 * NB], zt[b][0:NB, 0, :, c], d1[0:NB, 0, :], True, False, [zt[b], d1], [p])
                        self.mm(p[:, cc * 2 * NB:(cc + 1) * 2 * NB], zt[b][0:NB, 1, :, c], d1[0:NB, 1, :], False, True, [zt[b], d1], [p])
                    pv = p[:, 0:cpb * 2 * NB].rearrange("p (c r t) -> p c r t", r=2, t=NB)
                    for ri in range(2):
                        self.cp(Y[b][:, ri, :, cg:cg + cpb].rearrange("p t c -> p c t"), pv[:, :, ri, :], [p], [Y[b]], eng="act" if ri else "dve")
                for t in range(NB):
                    p = ps[4 + t % 4]
                    tpb = 512 // CB
                    self.mm(p[:, 0:CB], E[:, t, 0, :], Y[b][:, 0, t, :], True, False, [E, Y[b]], [p])
                    self.mm(p[:, 0:CB], E[:, t, 1, :], Y[b][:, 1, t, :], False, True, [E, Y[b]], [p])
                    self.cp(fo[b][:, t, :], p[:, 0:CB], [p], [fo[b]], eng="act" if t % 2 else "dve")
                for r0_ in range(0, 128, 16):
                    self.stq(fo[b], dp["f"][o + r0_ * NB:o + (r0_ + 16) * NB, c0:c0 + CB].rearrange("(r t) c -> r t c", t=NB), fo[b][r0_:r0_ + 16, :, :])
            em.barrier()


def _consts(S0, S1):
    c = {}
    c["c_ident"] = np.eye(128, dtype=np.float32).astype(ml_dtypes.bfloat16)
    i = np.arange(128)
    t, l = i[:, None], i[None, :]
    c["c_tri"] = np.stack([(t <= l), (t >= l), (t > l), (t < l)]).astype(np.float32)
    c["c_ones"] = np.ones((128, 128), np.float32)
    c["c_idf"] = np.eye(128, dtype=np.float32)
    S = max(S0, S1)
    inv = (10000.0 ** (-np.arange(0, 32, 2, dtype=np.float32) / 32)).astype(np.float32)
    ang = np.arange(S, dtype=np.float32)[:, None] * inv[None, :]
    c["c_rope"] = np.concatenate([np.cos(ang), np.sin(ang)], axis=1).astype(np.float32)
    cc = np.arange(256)
    th = 2 * np.pi * (cc[:, None] * cc[None, :] % 256) / 256
    c["c_cs"] = (np.concatenate([np.cos(th), -np.sin(th)], axis=1) / 16.0).astype(np.float32)
    for k, Sx in enumerate([S0, S1]):
        NB = Sx // 128
        b = np.arange(NB)
        ph = 2 * np.pi * (b[:, None] * b[None, :] % NB) / NB
        c[f"c_d1_{k}"] = np.stack([np.concatenate([np.cos(ph), -np.sin(ph)], 1), np.concatenate([np.sin(ph), np.cos(ph)], 1)]).astype(np.float32)
        q = np.arange(128, dtype=np.int64)
        m = (NB * np.arange(128)[None, None, :] + np.arange(NB)[None, :, None])
        th = 2 * np.pi * ((q[:, None, None] * m) % Sx) / Sx
        e = np.stack([np.cos(th), np.sin(th)], axis=2) / math.sqrt(Sx)
        c[f"c_e_{k}"] = e.astype(np.float32).astype(ml_dtypes.bfloat16)
    return c


_W = ["norm_mix_pre", "norm_mix_post", "norm_xa_pre", "norm_xa_post", "norm_mem", "xa_wq", "xa_wkv", "xa_wo", "norm_ffn_pre",
      "norm_ffn_post", "ffn_w_gu", "ffn_w_down", "ev_w_in", "ev_conv_w", "ev_conv_b", "ev_a_log_f", "ev_a_log_b", "ev_dt_bias_f",
      "ev_dt_bias_b", "ev_d_skip", "ev_ssm_norm", "ev_q_norm", "ev_w_uq", "ev_kv_norm", "ev_w_ukv", "ev_w_out", "od_w_mix"]


def make_in_maps(inputs, NC):
    xp = np.asarray(inputs["x_prompt"], np.float32)[0]
    xs = np.asarray(inputs["x_sample"], np.float32)
    mp = np.asarray(inputs["mem_prompt"], np.float32)[0]
    ms = np.asarray(inputs["mem_sample"], np.float32)
    S0, S1 = xs.shape[1], xp.shape[0]
    cst = _consts(S0, S1)
    w = {k: np.ascontiguousarray(np.asarray(inputs[k], np.float32)) for k in _W}
    maps = []
    per = S1 // NC
    for c in range(NC):
        m = dict(w)
        m.update(cst)
        m["x"] = np.ascontiguousarray(np.concatenate([xs[c], xp], axis=0))
        m["mem"] = np.ascontiguousarray(np.stack([ms[c], mp]))
        maps.append(m)
    return maps, S0, S1


_NC_CACHE = {}


def kernel(**inputs):
    NC = 8
    maps, S0, S1 = make_in_maps(inputs, NC)
    key = (S0, S1, NC)
    if key not in _NC_CACHE:
        _NC_CACHE[key] = Prog(S0, S1, NC).build()
    nc = _NC_CACHE[key]
    res = run_bass_kernel_spmd(nc, maps, core_ids=list(range(NC)))
    ys = np.stack([np.asarray(r["y0"], np.float32) for r in res.results])
    per = S1 // NC
    yp = np.concatenate([np.asarray(res.results[c]["y1"], np.float32)[c * per:(c + 1) * per] for c in range(NC)], axis=0)[None]
    return (yp, ys)
```

```python
import contextlib, math
import numpy as np
import ml_dtypes
import concourse.bass as bass
import concourse.mybir as mybir
from concourse.bass_utils import run_bass_kernel_spmd

F32 = mybir.dt.float32
BF16 = mybir.dt.bfloat16
U8 = mybir.dt.uint8
U32 = mybir.dt.uint32
AF = mybir.ActivationFunctionType
ALU = mybir.AluOpType
AX = mybir.AxisListType

D = 1024
NMEM = 256
DFF = 2816
DIN = 1968
EPS = 1e-6
ARENA = 200 * 1024
EP = 30000


class T:
    def __init__(self, ap):
        self.ap = ap
        self.w = None
        self.r = []

    def __getitem__(self, k):
        return self.ap[k]


class Node:
    __slots__ = ("eng", "fn", "deps", "sig", "dma", "semv", "nonc")

    def __init__(self, eng, fn, deps, dma=False):
        self.eng, self.fn, self.deps, self.dma = eng, fn, deps, dma
        self.sig = False
        self.semv = None
        self.nonc = False


class Em:
    ENGS = ["pe", "act", "dve", "pool", "sp"]

    def __init__(self, nc, es):
        self.nc, self.es = nc, es
        self.q = {e: [] for e in self.ENGS}
        self.dsem = {}
        self.npool = {"sp": 28, "pool": 20, "act": 8}
        for qn, n in self.npool.items():
            self.dsem[qn] = [[es.enter_context(nc.semaphore(f"d_{qn}{i}")), 0, None] for i in range(n)]
        self.drr = {qn: 0 for qn in self.npool}
        self.esem = {e: [] for e in self.ENGS}
        self.outstanding = []

    def _deps(self, reads, writes):
        deps = []
        for t in reads:
            if t.w is not None:
                deps.append(t.w)
        for t in writes:
            if t.w is not None:
                deps.append(t.w)
            deps.extend(t.r)
        return deps

    def _track(self, node, reads, writes):
        for t in reads:
            if not node.dma:
                t.r = [n for n in t.r if n.dma or n.eng != node.eng]
            t.r.append(node)
        for t in writes:
            t.w = node
            t.r = []
        for d in node.deps:
            if not (d.eng == "pe" and node.eng == "pe" and not d.dma and not node.dma):
                d.sig = True

    def op(self, eng, fn, reads=(), writes=()):
        node = Node(eng, fn, self._deps(reads, writes))
        self._track(node, reads, writes)
        self.q[eng].append(node)
        return node

    def dma(self, qn, out, in_, reads=(), writes=(), nonc=False):
        slot = self.dsem[qn][self.drr[qn]]
        self.drr[qn] = (self.drr[qn] + 1) % self.npool[qn]
        deps = self._deps(reads, writes)
        if slot[2] is not None:
            deps.append(slot[2])
        node = Node(qn, lambda e: e.dma_start(out=out, in_=in_), deps, dma=True)
        node.nonc = nonc
        slot[1] += 16
        node.semv = (slot[0], slot[1])
        slot[2] = node
        self._track(node, reads, writes)
        self.q[qn].append(node)
        self.outstanding.append(node)
        return node

    def gather(self, out, in_, idx_ap, reads=(), writes=()):
        qn = "pool"
        slot = self.dsem[qn][self.drr[qn]]
        self.drr[qn] = (self.drr[qn] + 1) % self.npool[qn]
        deps = self._deps(reads, writes)
        if slot[2] is not None:
            deps.append(slot[2])
        node = Node(qn, lambda e: e.indirect_dma_start(out=out, out_offset=None, in_=in_, in_offset=bass.IndirectOffsetOnAxis(ap=idx_ap, axis=0)), deps, dma=True)
        slot[1] += 16
        node.semv = (slot[0], slot[1])
        slot[2] = node
        self._track(node, reads, writes)
        self.q[qn].append(node)
        self.outstanding.append(node)
        return node

    def barrier(self):
        last = [self.q[e][-1] for e in self.ENGS if self.q[e]]
        deps = [n for n in last if n.fn is not None or True] + self.outstanding
        for e in self.ENGS:
            node = Node(e, None, list(deps))
            for d in deps:
                d.sig = True
            self.q[e].append(node)
        self.outstanding = []

    def finalize(self, block):
        nc, es = self.nc, self.es
        for e in self.ENGS:
            cnt = 0
            for node in self.q[e]:
                if node.dma or not node.sig or node.fn is None:
                    continue
                cnt += 1
                ep = (cnt - 1) // EP
                while len(self.esem[e]) <= ep:
                    self.esem[e].append(es.enter_context(nc.semaphore(f"e_{e}{len(self.esem[e])}")))
                node.semv = (self.esem[e][ep], (cnt - 1) % EP + 1)
        for e in self.ENGS:
            prev = None
            for node in self.q[e]:
                if node.fn is None:
                    node.semv = prev
                elif node.semv is not None:
                    prev = node.semv
        q = self.q

        def emit(ename, eng):
            known = {}
            for node in q[ename]:
                for d in node.deps:
                    if d.semv is None:
                        continue
                    if d.eng == "pe" and ename == "pe" and not d.dma and not node.dma:
                        continue
                    sem, val = d.semv
                    key = id(sem)
                    if known.get(key, 0) >= val:
                        continue
                    eng.wait_ge(sem, val)
                    known[key] = val
                if node.fn is None:
                    continue
                if node.nonc:
                    with nc.allow_non_contiguous_dma(reason="small strided vector load"):
                        ins = node.fn(eng)
                else:
                    ins = node.fn(eng)
                if node.dma:
                    ins.then_inc(node.semv[0], 16)
                elif node.sig:
                    ins.then_inc(node.semv[0], 1)

        @block.tensor
        def _(eng):
            emit("pe", eng)

        @block.scalar
        def _(eng):
            emit("act", eng)

        @block.vector
        def _(eng):
            emit("dve", eng)

        @block.gpsimd
        def _(eng):
            emit("pool", eng)

        @block.sync
        def _(eng):
            emit("sp", eng)


class Arena:
    def __init__(self, ap, nbytes):
        self.ap, self.n, self.off = ap, nbytes, 0

    def reset(self):
        self.off = 0

    def alloc(self, free, dt, parts=128):
        esz = {F32: 4, BF16: 2, U32: 4}[dt]
        n0 = int(np.prod(free)) * esz
        n = (n0 + 63) // 64 * 64
        assert self.off + n <= self.n, (self.off, n, self.n)
        ap = self.ap[0:parts, self.off:self.off + n0].bitcast(dt)
        self.off += n
        if len(free) == 2:
            ap = ap.rearrange("p (a b) -> p a b", b=free[1])
        elif len(free) == 3:
            ap = ap.rearrange("p (a b c) -> p a b c", b=free[1], c=free[2])
        return T(ap)


def bc(ap, shape, axis):
    return ap.unsqueeze(axis).to_broadcast(list(shape))


class Prog:
    def __init__(self, S0, S1, NC):
        self.S0, self.S1, self.NC = S0, S1, NC
        self.ST = S0 + S1
        self.seqs = [(0, S0), (S0, S1)]
        assert S1 % NC == 0

    def mm(self, out, lhsT, rhs, start, stop, reads, writes):
        self.em.op("pe", lambda e: e.matmul(out, lhsT=lhsT, rhs=rhs, start=start, stop=stop), reads, writes)

    def tr(self, out, in_, reads, writes):
        idb = self.identb
        n = in_.shape[0]
        self.em.op("pe", lambda e: e.transpose(out=out, in_=in_, identity=idb[0:n, 0:n]), list(reads) + [self.c_ident], writes)

    def act(self, out, in_, func, reads, writes, bias=None, scale=1.0, accum=None):
        kw = {}
        if bias is not None:
            kw["bias"] = bias
        if accum is not None:
            kw["accum_out"] = accum
        self.em.op("act", lambda e: e.activation(out=out, in_=in_, func=func, scale=scale, **kw), reads, writes)

    def tt(self, out, in0, in1, op, reads, writes, eng="dve"):
        self.em.op(eng, lambda e: e.tensor_tensor(out=out, in0=in0, in1=in1, op=op), reads, writes)

    def ts(self, out, in0, s1, s2, op0, op1, reads, writes, eng="dve"):
        if s2 is None:
            self.em.op(eng, lambda e: e.tensor_scalar(out=out, in0=in0, scalar1=s1, scalar2=None, op0=op0), reads, writes)
        else:
            self.em.op(eng, lambda e: e.tensor_scalar(out=out, in0=in0, scalar1=s1, scalar2=s2, op0=op0, op1=op1), reads, writes)

    def stt(self, out, in0, scalar, in1, op0, op1, reads, writes, eng="dve"):
        self.em.op(eng, lambda e: e.scalar_tensor_tensor(out=out, in0=in0, scalar=scalar, in1=in1, op0=op0, op1=op1), reads, writes)

    def cp(self, out, in_, reads, writes, eng="dve"):
        if eng == "act":
            eng = "dve"
        if False:
            pass
        else:
            self.em.op(eng, lambda e: e.tensor_copy(out=out, in_=in_), reads, writes)

    def ld(self, t, out, in_, q="sp", nonc=False):
        self.em.dma(q, out, in_, reads=(), writes=[t], nonc=nonc)

    def stq(self, t, out, in_, q="sp"):
        self.em.dma(q, out, in_, reads=[t], writes=())

    def rstd_from_ss(self, ss, rs, n, dim):
        self.act(rs[:, 0:n], ss[:, 0:n], AF.Ln, [ss, self.c_eps], [rs], bias=self.epsb[:, 0:1], scale=1.0 / dim)
        self.act(rs[:, 0:n], rs[:, 0:n], AF.Exp, [rs], [rs], scale=-0.5)

    def load_w(self, A, w_ap, K, N, name=None):
        kc = (K + 127) // 128
        t = A.alloc([kc, N], BF16)
        if K % 128 == 0:
            src = w_ap.rearrange("(k p) n -> p k n", p=128)
            for k in range(kc):
                for n0 in range(0, N, 2048):
                    n1 = min(N, n0 + 2048)
                    self.em.dma("pool", t[:, k, n0:n1], src[:, k, n0:n1], writes=[t])
        else:
            assert K < 128
            self.em.dma("pool", t[0:K, 0, :], w_ap, writes=[t])
        return t

    def load_vec_fm(self, A, v_ap, n):
        t = A.alloc([n // 128], F32)
        self.ld(t, t[:, :], v_ap.rearrange("(k p) -> p k", p=128), nonc=True)
        return t

    def load_vec_bc(self, A, v_ap, n):
        t = A.alloc([n], F32)
        self.ld(t, t[:, :], v_ap.partition_broadcast(128))
        return t

    def norm_T(self, A, x, xin_reads, dim, wfm, outT, psb, tmp_b, ss, rs, junk):
        kc = dim // 128
        self.act(junk[:, 0:dim], x, AF.Square, xin_reads, [junk, ss], accum=ss[:, 0:1])
        self.rstd_from_ss(ss, rs, 1, dim)
        self.ts(tmp_b[:, 0:dim], x, rs[:, 0:1], None, ALU.mult, None, list(xin_reads) + [rs], [tmp_b])
        pv = psb.ap.bitcast(BF16)
        for k in range(kc):
            self.tr(pv[:, k * 128:(k + 1) * 128], tmp_b[:, k * 128:(k + 1) * 128], [tmp_b], [psb])
        return pv

    def build(self):
        S0, S1, ST, NC = self.S0, self.S1, self.ST, self.NC
        nc = bass.Bass("TRN2", target_bir_lowering=False)
        self.nc = nc
        dp = {}

        def din(name, shape, dt=F32):
            dp[name] = nc.dram_tensor(name, list(shape), dt, kind="ExternalInput").ap()
            return dp[name]

        def dscr(name, shape, dt):
            dp[name] = nc.dram_tensor(name, list(shape), dt).ap()
            return dp[name]

        din("x", [ST, D]); din("mem", [2, NMEM, D])
        for nm in ["norm_mix_pre", "norm_mix_post", "norm_xa_pre", "norm_xa_post", "norm_mem", "norm_ffn_pre", "norm_ffn_post"]:
            din(nm, [2, D])
        din("xa_wq", [2, D, D]); din("xa_wkv", [2, D, 2 * D]); din("xa_wo", [2, D, D])
        din("ffn_w_gu", [2, D, 2 * DFF]); din("ffn_w_down", [2, DFF, D])
        din("ev_w_in", [1, D, DIN]); din("ev_conv_w", [1, 5, D]); din("ev_conv_b", [1, D])
        for nm in ["ev_a_log_f", "ev_a_log_b", "ev_dt_bias_f", "ev_dt_bias_b", "ev_d_skip"]:
            din(nm, [1, 8])
        din("ev_ssm_norm", [1, 512]); din("ev_q_norm", [1, 256]); din("ev_w_uq", [1, 256, 768])
        din("ev_kv_norm", [1, 128]); din("ev_w_ukv", [1, 128, 1024]); din("ev_w_out", [1, D, D])
        din("od_w_mix", [1, D, D])
        din("c_ident", [128, 128], BF16); din("c_tri", [4, 128, 128]); din("c_ones", [128, 128]); din("c_idf", [128, 128])
        din("c_rope", [max(S0, S1), 32]); din("c_cs", [256, 512])
        for i, S in enumerate([S0, S1]):
            NB = S // 128
            din(f"c_d1_{i}", [2, NB, 2 * NB]); din(f"c_e_{i}", [128, NB, 2, 128], BF16)
        yo0 = nc.dram_tensor("y0", [S0, D], F32, kind="ExternalOutput").ap()
        self.per = S1 // NC
        yo1 = nc.dram_tensor("y1", [self.per, D], F32, kind="ExternalOutput").ap()
        din("c_own", [128, max(1, self.per // 128)], U32)
        dscr("h_own", [self.per, D], F32)
        dscr("h", [ST, D], F32); dscr("z", [ST, 512], F32); dscr("xbcT", [D, ST + 4], F32)
        dscr("dtp", [ST, 32], F32); dscr("cq", [ST, 256], F32); dscr("ckv", [ST, 160], F32)
        dscr("xtm", [ST, 512], BF16); dscr("btm", [ST, 256], BF16); dscr("BT", [256, ST], BF16); dscr("CT", [256, ST], BF16)
        dscr("Hst", [2, ST // 128, 128, 512], BF16)
        dscr("QT", [8, 128, ST], BF16); dscr("KT", [8, 128, ST], BF16); dscr("V", [ST, 8, 128], BF16)
        dscr("mixT", [D, ST], BF16)
        dscr("Zr", [ST, D], BF16); dscr("Zi", [ST, D], BF16); dscr("f", [ST, D], BF16)
        self.dp = dp

        with contextlib.ExitStack() as es:
            arena = es.enter_context(nc.sbuf_tensor("arena", [128, ARENA], U8))
            cst = es.enter_context(nc.sbuf_tensor("cst", [128, 6 * 1024], U8))
            psall = es.enter_context(nc.psum_tensor("psall", [128, 8, 512], F32))
            self.psall = psall
            ps = [T(psall[:, i, :]) for i in range(8)]
            self.ps = ps
            em = Em(nc, es)
            self.em = em
            A = Arena(arena, ARENA)
            self.A = A
            C = Arena(cst, 6 * 1024)
            self.c_ident = C.alloc([128], BF16); self.identb = self.c_ident.ap
            self.ld(self.c_ident, self.identb, dp["c_ident"])
            self.c_eps = C.alloc([1], F32); self.epsb = self.c_eps.ap
            em.op("dve", lambda e: e.memset(self.epsb, EPS), (), [self.c_eps])
            self.c_tri = C.alloc([4, 128], F32)
            self.ld(self.c_tri, self.c_tri[:, :, :], dp["c_tri"].rearrange("a p n -> p a n"))
            self.c_ones = C.alloc([128], F32)
            self.ld(self.c_ones, self.c_ones[:, :], dp["c_ones"])
            self.c_idf = C.alloc([128], F32); self.idf = self.c_idf.ap
            self.ld(self.c_idf, self.idf, dp["c_idf"])
            self.c_onesb = C.alloc([128], BF16)
            self.cp(self.c_onesb[:, :], self.c_ones[:, :], [self.c_ones], [self.c_onesb])

            stages = [self.stage_A, self.stage_conv, self.stage_mla_prep, self.stage_ssd1, self.stage_ssd2, self.stage_attn,
                      lambda: self.stage_mixout(0), lambda: self.stage_ffn(0, None, None), self.stage_F, self.stage_dft,
                      lambda: self.stage_mixout(1), lambda: self.stage_ffn(1, yo0, yo1)]
            for st_ in stages[:getattr(self, "nstages", 12)]:
                st_()
            em.barrier()
            block = es.enter_context(nc.Block())
            em.finalize(block)
        return nc

    def stage_A(self):
        em, A, dp, ps = self.em, self.A, self.dp, self.ps
        A.reset()
        w_in = self.load_w(A, dp["ev_w_in"][0], D, DIN)
        wpre = self.load_vec_fm(A, dp["norm_mix_pre"][0], D)
        dtb = A.alloc([16], F32)
        self.ld(dtb, dtb[:, 0:8], dp["ev_dt_bias_f"][0].partition_broadcast(128))
        self.ld(dtb, dtb[:, 8:16], dp["ev_dt_bias_b"][0].partition_broadcast(128))
        alog = A.alloc([16], F32)
        self.ld(alog, alog[:, 0:8], dp["ev_a_log_f"][0].partition_broadcast(128))
        self.ld(alog, alog[:, 8:16], dp["ev_a_log_b"][0].partition_broadcast(128))
        aneg = A.alloc([16], F32)
        self.act(aneg[:, :], alog[:, :], AF.Exp, [alog], [aneg])
        self.ts(aneg[:, :], aneg[:, :], -1.0, None, ALU.mult, None, [aneg], [aneg])
        NB_ = 2
        xt = [A.alloc([D], F32) for _ in range(NB_)]
        xb = [A.alloc([D], BF16) for _ in range(NB_)]
        junk = A.alloc([D], F32)
        xT = [A.alloc([8, 128], BF16) for _ in range(NB_)]
        ss = [A.alloc([2], F32) for _ in range(NB_)]
        rs = [A.alloc([2], F32) for _ in range(NB_)]
        fo = [A.alloc([128], F32) for _ in range(4)]
        zo = [A.alloc([512], F32) for _ in range(NB_)]
        so = [A.alloc([416], F32) for _ in range(NB_)]
        dto = [A.alloc([32], F32) for _ in range(NB_)]
        dtt = [A.alloc([16], F32) for _ in range(NB_)]
        nt = self.ST // 128
        fi = 0
        for i in range(nt):
            b = i % NB_
            r0 = i * 128
            self.ld(xt[b], xt[b][:, :], dp["x"][r0:r0 + 128, :])
            pv = self.norm_T(A, xt[b][:, :], [xt[b]], D, wpre, None, ps[0], xb[b], ss[b], rs[b], junk)
            self.tt(xT[b][:, :, :], pv[:, 0:1024].rearrange("p (k t) -> p k t", t=128), bc(wpre[:, :], [128, 8, 128], 2), ALU.mult,
                    [ps[0], wpre], [xT[b]])
            for k in range(8):
                self.mm(ps[1][:, :], xT[b][:, k, :], w_in[:, k, 0:512], k == 0, k == 7, [xT[b], w_in], [ps[1]])
            self.cp(zo[b][:, :], ps[1][:, :], [ps[1]], [zo[b]], eng="act" if False else "dve")
            self.stq(zo[b], dp["z"][r0:r0 + 128, :], zo[b][:, :])
            for k in range(8):
                self.mm(ps[2][:, 0:432], xT[b][:, k, :], w_in[:, k, 1536:1968], k == 0, k == 7, [xT[b], w_in], [ps[2]])
            self.cp(so[b][:, :], ps[2][:, 16:432], [ps[2]], [so[b]])
            self.stq(so[b], dp["cq"][r0:r0 + 128, :], so[b][:, 0:256])
            self.stq(so[b], dp["ckv"][r0:r0 + 128, :], so[b][:, 256:416])
            self.tt(dtt[b][:, :], ps[2][:, 0:16], dtb[:, :], ALU.add, [ps[2], dtb], [dtt[b]])
            self.act(dtt[b][:, :], dtt[b][:, :], AF.Exp, [dtt[b]], [dtt[b]])
            self.act(dto[b][:, 0:16], dtt[b][:, :], AF.Ln, [dtt[b], self.c_ones], [dto[b]], bias=self.c_ones[:, 0:1])
            self.tt(dto[b][:, 16:32], dto[b][:, 0:16], aneg[:, :], ALU.mult, [dto[b], aneg], [dto[b]])
            self.stq(dto[b], dp["dtp"][r0:r0 + 128, :], dto[b][:, :])
            for c in range(8):
                p = ps[3 + (c % 4)]
                for k in range(8):
                    self.mm(p[:, 0:128], w_in[:, k, 512 + c * 128:512 + (c + 1) * 128], xT[b][:, k, :], k == 0, k == 7, [xT[b], w_in], [p])
                f = fo[fi % 4]; fi += 1
                self.cp(f[:, :], p[:, 0:128], [p], [f], eng="act" if c % 2 else "dve")
                self.stq(f, dp["xbcT"][c * 128:(c + 1) * 128, 2 + r0:2 + r0 + 128], f[:, :])
        em.barrier()

    def stage_conv(self):
        em, A, dp, ps = self.em, self.A, self.dp, self.ps
        A.reset()
        cw = A.alloc([5, 8], F32)
        for k in range(5):
            self.ld(cw, cw[:, k, :], dp["ev_conv_w"][0, k].rearrange("(c p) -> p c", p=128), nonc=True)
        cb = self.load_vec_fm(A, dp["ev_conv_b"][0], D)
        TB = 512
        NB_ = 2
        xin = [A.alloc([TB + 4], F32) for _ in range(NB_)]
        acc = [A.alloc([TB], F32) for _ in range(NB_)]
        sg = [A.alloc([TB], F32) for _ in range(NB_)]
        ob = [A.alloc([TB], BF16) for _ in range(NB_)]
        tmo = [A.alloc([4, 128], BF16) for _ in range(NB_)]
        it = 0
        for (o, S) in self.seqs:
            for c in range(8):
                for t0 in range(0, S, TB):
                    tb = min(TB, S - t0)
                    b = it % NB_; it += 1
                    lo = 0 if t0 > 0 else 2
                    hi = tb + 4 if t0 + tb < S else tb + 2
                    if lo or hi < tb + 4:
                        em.op("pool", (lambda e, ap=xin[b][:, 0:tb + 4]: e.memset(ap, 0.0)), (), [xin[b]])
                    self.ld(xin[b], xin[b][:, lo:hi], dp["xbcT"][c * 128:(c + 1) * 128, o + t0 + lo:o + t0 + hi])
                    self.ts(acc[b][:, 0:tb], xin[b][:, 0:tb], cw[:, 0, c:c + 1], cb[:, c:c + 1], ALU.mult, ALU.add, [xin[b], cw, cb], [acc[b]])
                    for k in range(1, 5):
                        self.stt(acc[b][:, 0:tb], xin[b][:, k:k + tb], cw[:, k, c:c + 1], acc[b][:, 0:tb], ALU.mult, ALU.add,
                                 [xin[b], cw, acc[b]], [acc[b]])
                    self.act(sg[b][:, 0:tb], acc[b][:, 0:tb], AF.Exp, [acc[b]], [sg[b]], scale=-1.0)
                    self.ts(sg[b][:, 0:tb], sg[b][:, 0:tb], 1.0, None, ALU.add, None, [sg[b]], [sg[b]])
                    em.op("dve", (lambda e, o_=sg[b][:, 0:tb]: e.reciprocal(out=o_, in_=o_)), [sg[b]], [sg[b]])
                    self.tt(ob[b][:, 0:tb], acc[b][:, 0:tb], sg[b][:, 0:tb], ALU.mult, [acc[b], sg[b]], [ob[b]])
                    if c >= 4:
                        dst = dp["BT"] if c < 6 else dp["CT"]
                        rr = (c - 4) % 2
                        self.stq(ob[b], dst[rr * 128:(rr + 1) * 128, o + t0:o + t0 + tb], ob[b][:, 0:tb])
                    if c < 6:
                        nsub = tb // 128
                        pv = ps[it % 2].ap.bitcast(BF16)
                        for j in range(nsub):
                            self.tr(pv[:, j * 128:(j + 1) * 128], ob[b][:, j * 128:(j + 1) * 128], [ob[b]], [ps[it % 2]])
                        self.cp(tmo[b][:, 0:nsub, :], pv[:, 0:nsub * 128].rearrange("p (j c) -> p j c", c=128), [ps[it % 2]], [tmo[b]], eng="act")
                        if c < 4:
                            dst = dp["xtm"][o + t0:o + t0 + tb, c * 128:(c + 1) * 128]
                        else:
                            dst = dp["btm"][o + t0:o + t0 + tb, (c - 4) * 128:(c - 3) * 128]
                        self.stq(tmo[b], dst.rearrange("(j p) c -> p j c", p=128), tmo[b][:, 0:nsub, :])
        em.barrier()

    def stage_mla_prep(self):
        em, A, dp, ps = self.em, self.A, self.dp, self.ps
        A.reset()
        w_uq = self.load_w(A, dp["ev_w_uq"][0], 256, 768)
        w_ukv = self.load_w(A, dp["ev_w_ukv"][0], 128, 1024)
        qn = self.load_vec_fm(A, dp["ev_q_norm"][0], 256)
        kn = self.load_vec_fm(A, dp["ev_kv_norm"][0], 128)
        kmax = A.alloc([8], F32)
        em.op("dve", lambda e: e.memset(kmax[:, :], 0.0), (), [kmax])
        kmb = A.alloc([1], F32)
        cin = A.alloc([256], F32); kin = A.alloc([160], F32); rope = A.alloc([32], F32)
        junk = A.alloc([256], F32); tb_ = A.alloc([256], BF16)
        ss = A.alloc([2], F32); rs = A.alloc([2], F32)
        cT = A.alloc([2, 128], BF16)
        qf = A.alloc([8, 128], F32); qb = A.alloc([8, 128], BF16); sq = A.alloc([8, 96], F32); n2 = A.alloc([8], F32)
        em.op("dve", lambda e: e.memset(qf[:, :, :], 0.0), (), [qf])
        r1 = A.alloc([8, 16], F32); r2 = A.alloc([8, 16], F32)
        vb = A.alloc([8, 128], BF16)
        em.op("dve", lambda e: e.memset(vb[:, :, :], 0.0), (), [vb])
        oT = [A.alloc([128], BF16) for _ in range(4)]
        nt = self.ST // 128

        def rope_apply(dst, src, heads, cosb, sinb, reads):
            x1, x2 = src[:, :, 0:16], src[:, :, 16:32]
            self.tt(r1[:, 0:heads, :], x1, cosb, ALU.mult, reads, [r1])
            self.tt(r2[:, 0:heads, :], x2, sinb, ALU.mult, reads, [r2])
            self.tt(dst[:, :, 0:16], r1[:, 0:heads, :], r2[:, 0:heads, :], ALU.subtract, [r1, r2], [qf])
            self.tt(r1[:, 0:heads, :], x2, cosb, ALU.mult, reads, [r1])
            self.tt(r2[:, 0:heads, :], x1, sinb, ALU.mult, reads, [r2])
            self.tt(dst[:, :, 16:32], r1[:, 0:heads, :], r2[:, 0:heads, :], ALU.add, [r1, r2], [qf])

        for phase in getattr(self, "mla_phases", (0, 1)):
            if phase == 1:
                em.op("dve", lambda e: e.memset(qf[:, :, 96:128], 0.0), (), [qf])
                em.op("dve", lambda e: e.reduce_max(out=ss[:, 0:1], in_=kmax[:, :], axis=AX.X), [kmax], [ss])
                self.mm(ps[0][0:1, 0:128], ss[:, 0:1], self.idf[:, :], True, True, [ss, self.c_idf], [ps[0]])
                em.op("dve", lambda e: e.reduce_max(out=rs[0:1, 0:1], in_=ps[0][0:1, 0:128], axis=AX.X), [ps[0]], [rs])
                self.mm(ps[1][:, 0:1], self.c_ones[0:1, :], rs[0:1, 0:1], True, True, [rs, self.c_ones], [ps[1]])
                self.act(kmb[:, :], ps[1][:, 0:1], AF.Ln, [ps[1]], [kmb])
                self.act(kmb[:, :], kmb[:, :], AF.Exp, [kmb], [kmb], scale=0.5)
                self.ts(kmb[:, :], kmb[:, :], -1.0, None, ALU.mult, None, [kmb], [kmb])
            for i in range(nt):
                r0 = i * 128
                seq = 0 if r0 < self.S0 else 1
                pos0 = r0 - self.seqs[seq][0]
                self.ld(rope, rope[:, :], dp["c_rope"][pos0:pos0 + 128, :])
                if phase == 0:
                    self.ld(kin, kin[:, :], dp["ckv"][r0:r0 + 128, :])
                    pv = self.norm_T(A, kin[:, 0:128], [kin], 128, kn, None, ps[0], tb_, ss, rs, junk)
                    self.ts(cT[:, 0, :], pv[:, 0:128], kn[:, 0:1], None, ALU.mult, None, [ps[0], kn], [cT])
                    if getattr(self, "dbg_cut", 9) <= 1:
                        continue
                    for hh in range(2):
                        self.mm(ps[1 + hh][:, :], cT[:, 0, :], w_ukv[:, 0, hh * 512:(hh + 1) * 512], True, True, [cT, w_ukv], [ps[1 + hh]])
                    for hh in range(2):
                        kvv = ps[1 + hh][:, :].rearrange("p (h c) -> p h c", c=128)
                        self.cp(qf[:, hh * 4:(hh + 1) * 4, 0:64], kvv[:, :, 0:64], [ps[1 + hh]], [qf])
                        self.cp(vb[:, hh * 4:(hh + 1) * 4, 0:64], kvv[:, :, 64:128], [ps[1 + hh]], [vb])
                    if getattr(self, "dbg_cut", 9) <= 2:
                        continue
                    em.op("dve", lambda e: e.memset(vb[:, :, 64:128], 1.0), (), [vb])
                    em.op("dve", lambda e: e.memset(qf[:, :, 96:128], 1.0), (), [qf])
                    kx1, kx2, cs_, sn_ = kin[:, 128:144], kin[:, 144:160], rope[:, 0:16], rope[:, 16:32]
                    self.tt(r1[:, 0, :], kx1, cs_, ALU.mult, [kin, rope], [r1])
                    self.tt(r2[:, 0, :], kx2, sn_, ALU.mult, [kin, rope], [r2])
                    self.tt(qf[:, 0, 64:80], r1[:, 0, :], r2[:, 0, :], ALU.subtract, [r1, r2], [qf])
                    self.tt(r1[:, 0, :], kx2, cs_, ALU.mult, [kin, rope], [r1])
                    self.tt(r2[:, 0, :], kx1, sn_, ALU.mult, [kin, rope], [r2])
                    self.tt(qf[:, 0, 80:96], r1[:, 0, :], r2[:, 0, :], ALU.add, [r1, r2], [qf])
                    for h in range(1, 8):
                        self.cp(qf[:, h, 64:96], qf[:, 0, 64:96], [qf], [qf])
                    if getattr(self, "dbg_cut", 9) <= 3:
                        continue
                    self.stq(vb, dp["V"][r0:r0 + 128, :, :], vb[:, :, :])
                    self.tt(sq[:, :, :], qf[:, :, 0:96], qf[:, :, 0:96], ALU.mult, [qf], [sq])
                    em.op("dve", lambda e: e.reduce_sum(out=n2[:, :], in_=sq[:, :, :], axis=AX.X), [sq], [n2])
                    self.tt(kmax[:, :], kmax[:, :], n2[:, :], ALU.max, [kmax, n2], [kmax])
                    dstT = dp["KT"]
                else:
                    self.ld(cin, cin[:, :], dp["cq"][r0:r0 + 128, :])
                    pv = self.norm_T(A, cin[:, :], [cin], 256, qn, None, ps[0], tb_, ss, rs, junk)
                    self.tt(cT[:, :, :], pv[:, 0:256].rearrange("p (k t) -> p k t", t=128), bc(qn[:, :], [128, 2, 128], 2), ALU.mult, [ps[0], qn], [cT])
                    for (c0, c1, p) in ((0, 480, ps[1]), (480, 768, ps[2])):
                        for k in range(2):
                            self.mm(p[:, 0:c1 - c0], cT[:, k, :], w_uq[:, k, c0:c1], k == 0, k == 1, [cT, w_uq], [p])
                    self.cp(qf[:, 0:5, 0:96], ps[1][:, 0:480].rearrange("p (h c) -> p h c", c=96), [ps[1]], [qf])
                    self.cp(qf[:, 5:8, 0:96], ps[2][:, 0:288].rearrange("p (h c) -> p h c", c=96), [ps[2]], [qf], eng="act")
                    self.tt(sq[:, :, :], qf[:, :, 0:96], qf[:, :, 0:96], ALU.mult, [qf], [sq])
                    em.op("dve", lambda e: e.reduce_sum(out=n2[:, :], in_=sq[:, :, :], axis=AX.X), [sq], [n2])
                    self.act(n2[:, :], n2[:, :], AF.Ln, [n2, self.c_eps], [n2], bias=self.epsb[:, 0:1])
                    self.act(n2[:, :], n2[:, :], AF.Exp, [n2], [n2], scale=0.5)
                    self.ts(qf[:, :, 96], n2[:, :], kmb[:, 0:1], None, ALU.mult, None, [n2, kmb], [qf])
                    self.cp(sq[:, :, 0:32], qf[:, :, 64:96], [qf], [sq])
                    rope_apply(qf[:, :, 64:96], sq[:, :, 0:32], 8, bc(rope[:, 0:16], [128, 8, 16], 1), bc(rope[:, 16:32], [128, 8, 16], 1), [sq, rope])
                    dstT = dp["QT"]
                if getattr(self, "dbg_cut", 9) <= 4:
                    continue
                self.cp(qb[:, :, :], qf[:, :, :], [qf], [qb], eng="act")
                if getattr(self, "dbg_cut", 9) <= 5:
                    continue
                for h in range(8):
                    p = ps[3 + (h % 4)]
                    pvv = p.ap.bitcast(BF16)
                    self.tr(pvv[:, 0:128], qb[:, h, :], [qb], [p])
                    o_ = oT[h % 4]
                    self.cp(o_[:, :], pvv[:, 0:128], [p], [o_], eng="act" if h % 2 else "dve")
                    self.stq(o_, dstT[h, :, r0:r0 + 128], o_[:, :])
        em.barrier()

    def stage_ssd1(self):
        em, A, dp, ps = self.em, self.A, self.dp, self.ps
        A.reset()
        tri = self.c_tri
        NB_ = 2
        bt = [A.alloc([256], BF16) for _ in range(NB_)]
        xt = [A.alloc([512], BF16) for _ in range(NB_)]
        dt = [A.alloc([32], F32) for _ in range(NB_)]
        H = [A.alloc([512], F32) for _ in range(2)]
        Hb = [A.alloc([512], BF16) for _ in range(4)]
        cd = A.alloc([16], F32); dst = A.alloc([16], F32); coef = A.alloc([8], F32)
        xs = A.alloc([512], BF16); tmp = A.alloc([512], F32)
        it = 0
        for (o, S) in self.seqs:
            ncnk = S // 128
            for d in (0, 1):
                em.op("dve", (lambda e, ap=H[d][:, :]: e.memset(ap, 0.0)), (), [H[d]])
            for step in range(ncnk):
                for d in (0, 1):
                    c = step if d == 0 else ncnk - 1 - step
                    b = it % NB_; hb = Hb[it % 4]; it += 1
                    r0 = o + c * 128
                    gc = r0 // 128
                    self.ld(bt[b], bt[b][:, :], dp["btm"][r0:r0 + 128, :])
                    self.ld(xt[b], xt[b][:, :], dp["xtm"][r0:r0 + 128, :])
                    self.ld(dt[b], dt[b][:, :], dp["dtp"][r0:r0 + 128, :])
                    adt = dt[b][:, 16 + 8 * d:24 + 8 * d]
                    self.mm(ps[0][:, 0:8], self.c_ones[:, :], adt, True, True, [self.c_ones, dt[b]], [ps[0]])
                    self.mm(ps[0][:, 8:16], tri[:, 2 + d, :], adt, True, True, [tri, dt[b]], [ps[0]])
                    self.act(cd[:, 0:16], ps[0][:, 0:16], AF.Exp, [ps[0]], [cd])
                    self.tt(coef[:, :], cd[:, 8:16], dt[b][:, 8 * d:8 * d + 8], ALU.mult, [cd, dt[b]], [coef])
                    self.tt(xs[:, :].rearrange("p (h c) -> p h c", c=64), xt[b][:, :].rearrange("p (h c) -> p h c", c=64),
                            bc(coef[:, :], [128, 8, 64], 2), ALU.mult, [xt[b], coef], [xs])
                    for g in range(2):
                        self.mm(ps[1][:, g * 256:(g + 1) * 256], bt[b][:, g * 128:(g + 1) * 128], xs[:, g * 256:(g + 1) * 256], True, True,
                                [bt[b], xs], [ps[1]])
                    self.cp(hb[:, :], H[d][:, :], [H[d]], [hb], eng="act")
                    self.stq(hb, dp["Hst"][d, gc], hb[:, :])
                    self.tt(tmp[:, :].rearrange("p (h c) -> p h c", c=64), H[d][:, :].rearrange("p (h c) -> p h c", c=64),
                            bc(cd[:, 0:8], [128, 8, 64], 2), ALU.mult, [H[d], cd], [tmp])
                    self.tt(H[d][:, :], tmp[:, :], ps[1][:, :], ALU.add, [tmp, ps[1]], [H[d]])
        em.barrier()

    def stage_ssd2(self):
        em, A, dp, ps = self.em, self.A, self.dp, self.ps
        A.reset()
        tri = self.c_tri
        dsk = A.alloc([8], F32)
        self.ld(dsk, dsk[:, :], dp["ev_d_skip"][0].partition_broadcast(128))
        snw = self.load_vec_bc(A, dp["ev_ssm_norm"][0], 512)
        NB_ = 2
        ct = [A.alloc([2, 128], BF16) for _ in range(NB_)]
        btT = [A.alloc([2, 128], BF16) for _ in range(NB_)]
        xt = [A.alloc([512], BF16) for _ in range(NB_)]
        zt = [A.alloc([512], F32) for _ in range(NB_)]
        dt = [A.alloc([32], F32) for _ in range(NB_)]
        hh = [A.alloc([2, 512], BF16) for _ in range(NB_)]
        cbm = A.alloc([4, 128], F32)
        rhs = A.alloc([4, 128], F32); dec = A.alloc([4, 128], F32)
        mT = A.alloc([16, 128], BF16)
        xdt = A.alloc([2, 512], BF16)
        ee = A.alloc([16], F32)
        y = A.alloc([512], F32); t1 = A.alloc([512], F32); sg = A.alloc([512], F32); junk = A.alloc([512], F32)
        ss = A.alloc([2], F32); rs = A.alloc([2], F32)
        yb = A.alloc([512], BF16); yT = [A.alloc([4, 128], BF16) for _ in range(2)]
        v3 = lambda ap: ap.rearrange("p (h c) -> p h c", c=64)
        nt = self.ST // 128
        for i in range(nt):
            b = i % NB_
            r0 = i * 128
            self.ld(ct[b], ct[b][:, :, :], dp["CT"][:, r0:r0 + 128].rearrange("(g p) t -> p g t", p=128))
            self.ld(btT[b], btT[b][:, :, :], dp["BT"][:, r0:r0 + 128].rearrange("(g p) t -> p g t", p=128))
            self.ld(xt[b], xt[b][:, :], dp["xtm"][r0:r0 + 128, :])
            self.ld(zt[b], zt[b][:, :], dp["z"][r0:r0 + 128, :])
            self.ld(dt[b], dt[b][:, :], dp["dtp"][r0:r0 + 128, :])
            self.ld(hh[b], hh[b][:, :, :], dp["Hst"][:, i].rearrange("d p c -> p d c"))
            for g in range(2):
                self.mm(ps[0][:, g * 128:(g + 1) * 128], btT[b][:, g, :], ct[b][:, g, :], True, True, [btT[b], ct[b]], [ps[0]])
            for d in range(2):
                self.tt(cbm[:, 2 * d:2 * d + 2, :], ps[0][:, 0:256].rearrange("p (g l) -> p g l", l=128), bc(tri[:, d, :], [128, 2, 128], 1),
                        ALU.mult, [ps[0], tri], [cbm])
            for d in range(2):
                self.mm(ps[1][:, d * 8:(d + 1) * 8], tri[:, d, :], dt[b][:, 16 + 8 * d:24 + 8 * d], True, True, [tri, dt[b]], [ps[1]])
            self.act(ee[:, :], ps[1][:, 0:16], AF.Exp, [ps[1]], [ee])
            for d in range(2):
                for g in range(2):
                    adt = dt[b][:, 16 + 8 * d + 4 * g:16 + 8 * d + 4 * g + 4]
                    self.tt(rhs[:, :, :], bc(tri[:, d, :], [128, 4, 128], 1), bc(adt, [128, 4, 128], 2), ALU.mult, [tri, dt[b]], [rhs])
                    p = ps[2 + (2 * d + g) % 2]
                    self.mm(p[:, :], tri[:, 2 + d, :], rhs[:, :, :].rearrange("p h l -> p (h l)"), True, True, [tri, rhs], [p])
                    self.act(dec[:, :, :].rearrange("p h l -> p (h l)"), p[:, :], AF.Exp, [p], [dec])
                    k0 = 8 * d + 4 * g
                    self.tt(mT[:, k0:k0 + 4, :], dec[:, :, :], bc(cbm[:, 2 * d + g, :], [128, 4, 128], 1), ALU.mult, [dec, cbm], [mT])
                self.tt(v3(xdt[:, d, :]), v3(xt[b][:, :]), bc(dt[b][:, 8 * d:8 * d + 8], [128, 8, 64], 2), ALU.mult, [xt[b], dt[b]], [xdt])
            for h in range(8):
                for d in range(2):
                    self.mm(ps[4][:, h * 64:(h + 1) * 64], mT[:, 8 * d + h, :], xdt[:, d, h * 64:(h + 1) * 64], d == 0, d == 1, [mT, xdt], [ps[4]])
            for d in range(2):
                for g in range(2):
                    self.mm(ps[5 + d][:, g * 256:(g + 1) * 256], ct[b][:, g, :], hh[b][:, d, g * 256:(g + 1) * 256], True, True, [ct[b], hh[b]], [ps[5 + d]])
            self.tt(v3(y[:, :]), v3(ps[5][:, :]), bc(ee[:, 0:8], [128, 8, 64], 2), ALU.mult, [ps[5], ee], [y])
            self.tt(v3(t1[:, :]), v3(ps[6][:, :]), bc(ee[:, 8:16], [128, 8, 64], 2), ALU.mult, [ps[6], ee], [t1])
            self.tt(y[:, :], y[:, :], t1[:, :], ALU.add, [y, t1], [y])
            self.tt(y[:, :], y[:, :], ps[4][:, :], ALU.add, [y, ps[4]], [y])
            self.tt(v3(t1[:, :]), v3(xt[b][:, :]), bc(dsk[:, :], [128, 8, 64], 2), ALU.mult, [xt[b], dsk], [t1])
            self.tt(y[:, :], y[:, :], t1[:, :], ALU.add, [y, t1], [y])
            self.act(sg[:, :], zt[b][:, :], AF.Exp, [zt[b]], [sg], scale=-1.0)
            self.ts(sg[:, :], sg[:, :], 1.0, None, ALU.add, None, [sg], [sg])
            em.op("dve", lambda e: e.reciprocal(out=sg[:, :], in_=sg[:, :]), [sg], [sg])
            self.tt(sg[:, :], sg[:, :], zt[b][:, :], ALU.mult, [sg, zt[b]], [sg])
            self.tt(y[:, :], y[:, :], sg[:, :], ALU.mult, [y, sg], [y])
            for g in range(2):
                self.act(junk[:, 0:256], y[:, g * 256:(g + 1) * 256], AF.Square, [y], [junk, ss], accum=ss[:, g:g + 1])
            self.rstd_from_ss(ss, rs, 2, 256)
            self.tt(y[:, :].rearrange("p (g c) -> p g c", c=256), y[:, :].rearrange("p (g c) -> p g c", c=256), bc(rs[:, 0:2], [128, 2, 256], 2),
                    ALU.mult, [y, rs], [y])
            self.tt(yb[:, :], y[:, :], snw[:, :], ALU.mult, [y, snw], [yb])
            pv = ps[7].ap.bitcast(BF16)
            for k in range(4):
                self.tr(pv[:, k * 128:(k + 1) * 128], yb[:, k * 128:(k + 1) * 128], [yb], [ps[7]])
            yo = yT[i % 2]
            self.cp(yo[:, :, :], pv[:, 0:512].rearrange("p (k t) -> p k t", t=128), [ps[7]], [yo], eng="act")
            self.stq(yo, dp["mixT"][0:512, r0:r0 + 128].rearrange("(k p) t -> p k t", p=128), yo[:, :, :])
        em.barrier()

    def stage_attn(self):
        em, A, dp, ps = self.em, self.A, self.dp, self.ps
        A.reset()
        scale = 96.0 ** -0.5
        for (o, S) in self.seqs:
            A.reset()
            nkt = S // 128
            QB = min(512, S)
            KT = [A.alloc([S], BF16) for _ in range(2)]
            VV = [A.alloc([nkt, 128], BF16) for _ in range(2)]
            qt = [A.alloc([QB], BF16) for _ in range(2)]
            pT = [A.alloc([2, QB], BF16) for _ in range(3)]
            osb = A.alloc([QB], F32); rc = A.alloc([QB], F32); ob = [A.alloc([QB], BF16) for _ in range(2)]
            it = 0
            for h in range(8):
                kt, vv = KT[h % 2], VV[h % 2]
                self.ld(kt, kt[:, :], dp["KT"][h, :, o:o + S])
                for k0 in range(0, nkt, 16):
                    k1 = min(nkt, k0 + 16)
                    self.ld(vv, vv[:, k0:k1, :], dp["V"][o + k0 * 128:o + k1 * 128, h, :].rearrange("(k p) c -> p k c", p=128))
                for q0 in range(0, S, QB):
                    qq = qt[(it // 1) % 2]
                    self.ld(qq, qq[:, :], dp["QT"][h, :, o + q0:o + q0 + QB])
                    po = ps[6 + it % 2]
                    for k in range(0, nkt, 2):
                        j = k % 4
                        pa, pb = ps[j], ps[j + 1]
                        self.mm(pa[:, 0:QB], kt[:, k * 128:(k + 1) * 128], qq[:, :], True, True, [kt, qq], [pa])
                        self.mm(pb[:, 0:QB], kt[:, (k + 1) * 128:(k + 2) * 128], qq[:, :], True, True, [kt, qq], [pb])
                        pt = pT[(k // 2) % 3]
                        if QB == 512:
                            self.act(pt[:, :, :].rearrange("p a b -> p (a b)"), self.psall[:, j:j + 2, :].rearrange("p a b -> p (a b)"), AF.Exp, [pa, pb], [pt], scale=scale)
                        else:
                            self.act(pt[:, 0, :], pa[:, 0:QB], AF.Exp, [pa], [pt], scale=scale)
                            self.act(pt[:, 1, :], pb[:, 0:QB], AF.Exp, [pb], [pt], scale=scale)
                        self.mm(po[:, 0:QB], vv[:, k, :], pt[:, 0, :], k == 0, False, [vv, pt], [po])
                        self.mm(po[:, 0:QB], vv[:, k + 1, :], pt[:, 1, :], False, k + 2 >= nkt, [vv, pt], [po])
                    self.cp(osb[:, :], po[:, 0:QB], [po], [osb])
                    em.op("dve", (lambda e, o_=rc[64:65, :], i_=osb[64:65, :]: e.reciprocal(out=o_, in_=i_)), [osb], [rc])
                    self.mm(ps[4 + it % 2][0:64, 0:QB], self.c_ones[64:65, 0:64], rc[64:65, :], True, True, [self.c_ones, rc], [ps[4 + it % 2]])
                    obb = ob[it % 2]
                    self.tt(obb[0:64, :], osb[0:64, :], ps[4 + it % 2][0:64, 0:QB], ALU.mult, [osb, ps[4 + it % 2]], [obb])
                    self.stq(obb, dp["mixT"][512 + h * 64:512 + (h + 1) * 64, o + q0:o + q0 + QB], obb[0:64, :])
                    it += 1
            em.barrier()

    def stage_mixout(self, L):
        em, A, dp, ps = self.em, self.A, self.dp, self.ps
        A.reset()
        wmix = self.load_w(A, dp["ev_w_out"][0] if L == 0 else dp["od_w_mix"][0], D, D)
        wq = self.load_w(A, dp["xa_wq"][L], D, D)
        wo = self.load_w(A, dp["xa_wo"][L], D, D)
        npost = self.load_vec_bc(A, dp["norm_mix_post"][L], D)
        nxpre = self.load_vec_fm(A, dp["norm_xa_pre"][L], D)
        nxpost = self.load_vec_bc(A, dp["norm_xa_post"][L], D)
        nmem = self.load_vec_fm(A, dp["norm_mem"][L], D)
        kT = [A.alloc([8, NMEM], BF16) for _ in range(2)]
        vm = [A.alloc([2, D], BF16) for _ in range(2)]
        kmx = [A.alloc([4], F32) for _ in range(2)]
        mt = A.alloc([D], F32); mb = A.alloc([D], BF16); junk = A.alloc([D], F32); mT = A.alloc([8, 128], BF16)
        ss = A.alloc([8], F32); rs = A.alloc([8], F32)
        kk = A.alloc([D], F32); k2 = A.alloc([4], F32); km = A.alloc([4], F32)
        wkv = A.alloc([8, 512], BF16)
        for s in range(2):
            em.op("dve", (lambda e, ap=km[:, :]: e.memset(ap, 0.0)), (), [km])
            for mc in range(2):
                self.ld(mt, mt[:, :], dp["mem"][s, mc * 128:(mc + 1) * 128, :])
                pv = self.norm_T(A, mt[:, :], [mt], D, nmem, None, ps[0], mb, ss, rs, junk)
                self.tt(mT[:, :, :], pv[:, 0:1024].rearrange("p (k t) -> p k t", t=128), bc(nmem[:, :], [128, 8, 128], 2), ALU.mult, [ps[0], nmem], [mT])
                for nq in range(4):
                    for k in range(8):
                        self.em.dma("pool", wkv[:, k, :], dp["xa_wkv"][L].rearrange("(k p) n -> p k n", p=128)[:, k, nq * 512:(nq + 1) * 512], writes=[wkv])
                    for k in range(8):
                        self.mm(ps[1][:, :], mT[:, k, :], wkv[:, k, :], k == 0, k == 7, [mT, wkv], [ps[1]])
                    if nq < 2:
                        self.cp(kk[:, nq * 512:(nq + 1) * 512], ps[1][:, :], [ps[1]], [kk])
                    else:
                        self.cp(vm[s][:, mc, (nq - 2) * 512:(nq - 1) * 512], ps[1][:, :], [ps[1]], [vm[s]], eng="act")
                self.tt(junk[:, :], kk[:, :], kk[:, :], ALU.mult, [kk], [junk])
                em.op("dve", lambda e: e.reduce_sum(out=k2[:, :], in_=junk[:, :].rearrange("p (h c) -> p h c", c=256), axis=AX.X), [junk], [k2])
                self.tt(km[:, :], km[:, :], k2[:, :], ALU.max, [km, k2], [km])
                self.cp(mb[:, :], kk[:, :], [kk], [mb], eng="act")
                pv = ps[2].ap.bitcast(BF16)
                for k in range(8):
                    self.tr(pv[:, k * 128:(k + 1) * 128], mb[:, k * 128:(k + 1) * 128], [mb], [ps[2]])
                self.cp(kT[s][:, :, mc * 128:(mc + 1) * 128], pv[:, 0:1024].rearrange("p (k t) -> p k t", t=128), [ps[2]], [kT[s]])
            self.mm(ps[3][0:4, 0:128], km[:, :], self.idf[:, :], True, True, [km, self.c_idf], [ps[3]])
            em.op("dve", lambda e: e.reduce_max(out=rs[0:4, 0:1], in_=ps[3][0:4, 0:128], axis=AX.X), [ps[3]], [rs])
            self.ts(ss[0:4, 0:4], self.idf[0:4, 0:4], rs[0:4, 0:1], None, ALU.mult, None, [rs, self.c_idf], [ss])
            self.mm(ps[3][:, 128:132], self.c_ones[0:4, :], ss[0:4, 0:4], True, True, [self.c_ones, ss], [ps[3]])
            self.act(kmx[s][:, :], ps[3][:, 128:132], AF.Ln, [ps[3]], [kmx[s]])
            self.act(kmx[s][:, :], kmx[s][:, :], AF.Exp, [kmx[s]], [kmx[s]], scale=0.5)
            self.ts(kmx[s][:, :], kmx[s][:, :], -1.0, None, ALU.mult, None, [kmx[s]], [kmx[s]])
        NB_ = 2
        NTOK = 512 if (min(self.S0, self.S1) % 512 == 0 and self.per % 512 == 0) else 256
        ht = [A.alloc([NTOK // 128, D], F32) for _ in range(NB_)]
        mi = [A.alloc([D], BF16) for _ in range(NB_)]
        miT = [A.alloc([8, NTOK], BF16) for _ in range(NB_)]
        hb = A.alloc([D], BF16)
        hT = A.alloc([8, NTOK], BF16)
        qT = A.alloc([8, NTOK], BF16); q2 = A.alloc([NTOK], BF16)
        shf = A.alloc([NTOK], F32)
        pT = A.alloc([2, NTOK], BF16)
        den = A.alloc([NTOK], F32)
        oT = A.alloc([8, NTOK], BF16)
        tmp = A.alloc([D], F32)
        xs = 256.0 ** -0.5
        own = A.alloc([max(1, self.per // 128)], U32)
        self.ld(own, own[:, :], dp["c_own"])
        for (si, (o, S)) in enumerate(self.seqs):
            gat = (L == 1 and si == 1)
            for t0 in range(0, self.per if gat else S, NTOK):
                b = (t0 // NTOK) % NB_
                r0 = o + t0
                nsub = NTOK // 128
                hsrc = dp["x"] if L == 0 else dp["h"]
                if gat:
                    for j in range(nsub):
                        gi = t0 // 128 + j
                        em.gather(ht[b][:, j, :], dp["h"], own[:, gi:gi + 1], reads=[own], writes=[ht[b]])
                else:
                    self.ld(ht[b], ht[b][:, :, :], hsrc[r0:r0 + NTOK, :].rearrange("(j p) c -> p j c", p=128))
                if L == 0:
                    self.ld(miT[b], miT[b][:, :, :], dp["mixT"][:, r0:r0 + NTOK].rearrange("(k p) t -> p k t", p=128))
                else:
                    for j in range(nsub):
                        if gat:
                            gi = t0 // 128 + j
                            em.gather(mi[b][:, :], dp["f"], own[:, gi:gi + 1], reads=[own], writes=[mi[b]])
                        else:
                            self.ld(mi[b], mi[b][:, :], dp["f"][r0 + j * 128:r0 + (j + 1) * 128, :])
                        pv = ps[0].ap.bitcast(BF16)
                        for k in range(8):
                            self.tr(pv[:, k * 128:(k + 1) * 128], mi[b][:, k * 128:(k + 1) * 128], [mi[b]], [ps[0]])
                        self.cp(miT[b][:, :, j * 128:(j + 1) * 128], pv[:, 0:1024].rearrange("p (k t) -> p k t", t=128), [ps[0]], [miT[b]])
                for j in range(nsub):
                    hj = ht[b][:, j, :]
                    for hf in range(2):
                        for k in range(8):
                            self.mm(ps[1 + hf][:, :], miT[b][:, k, j * 128:(j + 1) * 128], wmix[:, k, hf * 512:(hf + 1) * 512], k == 0, k == 7, [miT[b], wmix], [ps[1 + hf]])
                    self.post_norm_add(hj, ht[b], [ps[1], ps[2]], npost, ss, rs, junk, tmp)
                    pv = self.norm_T(A, hj, [ht[b]], D, nxpre, None, ps[0], hb, ss, rs, junk)
                    self.tt(hT[:, :, j * 128:(j + 1) * 128], pv[:, 0:1024].rearrange("p (k t) -> p k t", t=128), bc(nxpre[:, :], [128, 8, 128], 2), ALU.mult, [ps[0], nxpre], [hT])
                for c in range(8):
                    p = ps[3 + c % 2]
                    for k in range(8):
                        self.mm(p[:, 0:NTOK], wq[:, k, c * 128:(c + 1) * 128], hT[:, k, :], k == 0, k == 7, [hT, wq], [p])
                    self.cp(qT[:, c, :], p[:, 0:NTOK], [p], [qT], eng="act" if c % 2 else "dve")
                for h in range(4):
                    for cc in range(2):
                        self.tt(q2[:, :], qT[:, 2 * h + cc, :], qT[:, 2 * h + cc, :], ALU.mult, [qT], [q2])
                        self.mm(ps[5][:, 0:NTOK], self.c_onesb[:, :], q2[:, :], cc == 0, cc == 1, [self.c_onesb, q2], [ps[5]])
                    self.act(shf[:, :], ps[5][:, 0:NTOK], AF.Ln, [ps[5], self.c_eps], [shf], bias=self.epsb[:, 0:1])
                    self.act(shf[:, :], shf[:, :], AF.Exp, [shf], [shf], scale=0.5)
                    self.ts(shf[:, :], shf[:, :], kmx[si][:, h:h + 1], xs, ALU.mult, ALU.mult, [shf, kmx[si]], [shf])
                    for mc in range(2):
                        p = ps[6]
                        for cc in range(2):
                            self.mm(p[:, 0:NTOK], kT[si][:, 2 * h + cc, mc * 128:(mc + 1) * 128], qT[:, 2 * h + cc, :], cc == 0, cc == 1, [kT[si], qT], [p])
                        self.stt(den[:, :], p[:, 0:NTOK], xs, shf[:, :], ALU.mult, ALU.add, [p, shf], [den])
                        self.act(pT[:, mc, :], den[:, :], AF.Exp, [den], [pT])
                    for mc in range(2):
                        self.mm(ps[7][:, 0:NTOK], self.c_onesb[:, :], pT[:, mc, :], mc == 0, mc == 1, [self.c_onesb, pT], [ps[7]])
                    em.op("dve", lambda e: e.reciprocal(out=den[:, :], in_=ps[7][:, 0:NTOK]), [ps[7]], [den])
                    for cc in range(2):
                        p = ps[3 + cc]
                        for mc in range(2):
                            self.mm(p[:, 0:NTOK], vm[si][:, mc, (2 * h + cc) * 128:(2 * h + cc + 1) * 128], pT[:, mc, :], mc == 0, mc == 1, [vm[si], pT], [p])
                        self.tt(oT[:, 2 * h + cc, :], p[:, 0:NTOK], den[:, :], ALU.mult, [p, den], [oT])
                for j in range(nsub):
                    hj = ht[b][:, j, :]
                    for hf in range(2):
                        for k in range(8):
                            self.mm(ps[1 + hf][:, :], oT[:, k, j * 128:(j + 1) * 128], wo[:, k, hf * 512:(hf + 1) * 512], k == 0, k == 7, [oT, wo], [ps[1 + hf]])
                    self.post_norm_add(hj, ht[b], [ps[1], ps[2]], nxpost, ss, rs, junk, tmp)
                if gat:
                    self.stq(ht[b], dp["h_own"][t0:t0 + NTOK, :].rearrange("(j p) c -> p j c", p=128), ht[b][:, :, :])
                else:
                    self.stq(ht[b], dp["h"][r0:r0 + NTOK, :].rearrange("(j p) c -> p j c", p=128), ht[b][:, :, :])
        em.barrier()

    def post_norm_add(self, hj, ht, pss, wbc, ss, rs, junk, tmp):
        for hf in range(2):
            self.act(junk[:, hf * 512:(hf + 1) * 512], pss[hf][:, :], AF.Square, [pss[hf]], [junk, ss], accum=ss[:, hf:hf + 1])
        self.tt(ss[:, 0:1], ss[:, 0:1], ss[:, 1:2], ALU.add, [ss], [ss])
        self.rstd_from_ss(ss, rs, 1, D)
        for hf in range(2):
            self.tt(tmp[:, hf * 512:(hf + 1) * 512], pss[hf][:, :], wbc[:, hf * 512:(hf + 1) * 512], ALU.mult, [pss[hf], wbc], [tmp])
        self.stt(hj, tmp[:, :], rs[:, 0:1], hj, ALU.mult, ALU.add, [tmp, rs, ht], [ht])

    def stage_ffn(self, L, yo0, yo1):
        em, A, dp, ps = self.em, self.A, self.dp, self.ps
        A.reset()
        wgu = self.load_w(A, dp["ffn_w_gu"][L], D, 2 * DFF)
        wd = self.load_w(A, dp["ffn_w_down"][L], DFF, D)
        npre = self.load_vec_fm(A, dp["norm_ffn_pre"][L], D)
        npost = self.load_vec_bc(A, dp["norm_ffn_post"][L], D)
        NTOK = 512 if (min(self.S0, self.S1) % 512 == 0 and self.per % 512 == 0) else 256
        NB_ = 1
        ht = [A.alloc([NTOK // 128, D], F32) for _ in range(NB_)]
        hb = A.alloc([D], BF16); junk = A.alloc([D], BF16); tmp = A.alloc([D], F32)
        hT = A.alloc([8, NTOK], BF16)
        aT = A.alloc([22, NTOK], BF16)
        sg = [A.alloc([NTOK], F32) for _ in range(2)]
        ss = A.alloc([2], F32); rs = A.alloc([2], F32)
        it = 0
        for (si, (o, S)) in enumerate(self.seqs):
            own_ = (L == 1 and si == 1)
            for t0 in range(0, self.per if own_ else S, NTOK):
                b = it % NB_; it += 1
                r0 = o + t0
                nsub = NTOK // 128
                if own_:
                    self.ld(ht[b], ht[b][:, :, :], dp["h_own"][t0:t0 + NTOK, :].rearrange("(j p) c -> p j c", p=128))
                else:
                    self.ld(ht[b], ht[b][:, :, :], dp["h"][r0:r0 + NTOK, :].rearrange("(j p) c -> p j c", p=128))
                for j in range(nsub):
                    pv = self.norm_T(A, ht[b][:, j, :], [ht[b]], D, npre, None, ps[0], hb, ss, rs, junk)
                    self.tt(hT[:, :, j * 128:(j + 1) * 128], pv[:, 0:1024].rearrange("p (k t) -> p k t", t=128), bc(npre[:, :], [128, 8, 128], 2), ALU.mult, [ps[0], npre], [hT])
                for f in range(22):
                    pg, pu = ps[1 + 2 * (f % 2)], ps[2 + 2 * (f % 2)]
                    for k in range(8):
                        self.mm(pg[:, 0:NTOK], wgu[:, k, f * 128:(f + 1) * 128], hT[:, k, :], k == 0, k == 7, [hT, wgu], [pg])
                    for k in range(8):
                        self.mm(pu[:, 0:NTOK], wgu[:, k, DFF + f * 128:DFF + (f + 1) * 128], hT[:, k, :], k == 0, k == 7, [hT, wgu], [pu])
                    s_ = sg[f % 2]
                    self.act(s_[:, :], pg[:, 0:NTOK], AF.Exp, [pg], [s_], scale=-1.0)
                    self.ts(s_[:, :], s_[:, :], 1.0, None, ALU.add, None, [s_], [s_])
                    em.op("dve", (lambda e, a_=s_[:, :]: e.reciprocal(out=a_, in_=a_)), [s_], [s_])
                    self.tt(s_[:, :], s_[:, :], pg[:, 0:NTOK], ALU.mult, [s_, pg], [s_])
                    self.tt(aT[:, f, :], s_[:, :], pu[:, 0:NTOK], ALU.mult, [s_, pu], [aT])
                for j in range(nsub):
                    for hf in range(2):
                        for k in range(22):
                            self.mm(ps[5 + hf][:, :], aT[:, k, j * 128:(j + 1) * 128], wd[:, k, hf * 512:(hf + 1) * 512], k == 0, k == 21, [aT, wd], [ps[5 + hf]])
                    self.post_norm_add(ht[b][:, j, :], ht[b], [ps[5], ps[6]], npost, ss, rs, junk, tmp)
                if L == 0:
                    self.stq(ht[b], dp["h"][r0:r0 + NTOK, :].rearrange("(j p) c -> p j c", p=128), ht[b][:, :, :])
                elif si == 0:
                    self.stq(ht[b], yo0[t0:t0 + NTOK, :].rearrange("(j p) c -> p j c", p=128), ht[b][:, :, :])
                else:
                    self.stq(ht[b], yo1[t0:t0 + NTOK, :].rearrange("(j p) c -> p j c", p=128), ht[b][:, :, :])
        em.barrier()

    def stage_F(self):
        em, A, dp, ps = self.em, self.A, self.dp, self.ps
        A.reset()
        cs = self.load_w(A, dp["c_cs"], 256, 512)
        npre = self.load_vec_fm(A, dp["norm_mix_pre"][1], D)
        NB_ = 2
        ht = [A.alloc([D], F32) for _ in range(NB_)]
        hb = A.alloc([D], BF16); junk = A.alloc([D], F32)
        hT = A.alloc([8, 128], BF16)
        ss = A.alloc([2], F32); rs = A.alloc([2], F32)
        zo = [A.alloc([2, D], BF16) for _ in range(NB_)]
        for i in range(self.ST // 128):
            b = i % NB_
            r0 = i * 128
            self.ld(ht[b], ht[b][:, :], dp["h"][r0:r0 + 128, :])
            pv = self.norm_T(A, ht[b][:, :], [ht[b]], D, npre, None, ps[0], hb, ss, rs, junk)
            self.tt(hT[:, :, :], pv[:, 0:1024].rearrange("p (k t) -> p k t", t=128), bc(npre[:, :], [128, 8, 128], 2), ALU.mult, [ps[0], npre], [hT])
            for g in range(4):
                p = ps[1 + g % 4]
                for k in range(2):
                    self.mm(p[:, :], hT[:, 2 * g + k, :], cs[:, k, :], k == 0, k == 1, [hT, cs], [p])
                self.cp(zo[b][:, :, g * 256:(g + 1) * 256], p[:, :].rearrange("p (r c) -> p r c", c=256), [p], [zo[b]], eng="act" if g % 2 else "dve")
            self.stq(zo[b], dp["Zr"][r0:r0 + 128, :], zo[b][:, 0, :])
            self.stq(zo[b], dp["Zi"][r0:r0 + 128, :], zo[b][:, 1, :])
        em.barrier()

    def stage_dft(self):
        em, A, dp, ps = self.em, self.A, self.dp, self.ps
        CB = 64
        for (si, (o, S)) in enumerate(self.seqs):
            A.reset()
            NB = S // 128
            d1 = A.alloc([2, 2 * NB], BF16)
            self.em.dma("pool", d1[0:NB, :, :], dp[f"c_d1_{si}"].rearrange("a b n -> b a n"), writes=[d1])
            E = A.alloc([NB, 2, 128], BF16)
            self.ld(E, E[:, :, :, :], dp[f"c_e_{si}"])
            zt = [A.alloc([2, 128, CB], BF16) for _ in range(2)]
            nbuf = 2 if NB <= 64 else 1
            Y = [A.alloc([2, NB, CB], BF16) for _ in range(nbuf)] * (2 // nbuf)
            fo = [A.alloc([NB, CB], BF16) for _ in range(nbuf)] * (2 // nbuf)
            cpb = min(CB, max(1, 512 // (2 * NB)))
            for cb in range(D // CB):
                b = cb % 2
                c0 = cb * CB
                for ri, nm in enumerate(["Zr", "Zi"]):
                    for b0 in range(0, NB, 16):
                        b1 = min(NB, b0 + 16)
                        self.ld(zt[b], zt[b][b0:b1, ri, :, :], dp[nm][o + b0 * 128:o + b1 * 128, c0:c0 + CB].rearrange("(b q) c -> b q c", q=128))
                for cg in range(0, CB, cpb):
                    p = ps[(cg // cpb) % 4]
                    for cc in range(cpb):
                        c = cg + cc
                        self.mm(p[:, cc * 2 * NB:(cc + 1) * 2 * NB], zt[b][0:NB, 0, :, c], d1[0:NB, 0, :], True, False, [zt[b], d1], [p])
                        self.mm(p[:, cc * 2 * NB:(cc + 1) * 2 * NB], zt[b][0:NB, 1, :, c], d1[0:NB, 1, :], False, True, [zt[b], d1], [p])
                    pv = p[:, 0:cpb * 2 * NB].rearrange("p (c r t) -> p c r t", r=2, t=NB)
                    for ri in range(2):
                        self.cp(Y[b][:, ri, :, cg:cg + cpb].rearrange("p t c -> p c t"), pv[:, :, ri, :], [p], [Y[b]], eng="act" if ri else "dve")
                for t in range(NB):
                    p = ps[4 + t % 4]
                    tpb = 512 // CB
                    self.mm(p[:, 0:CB], E[:, t, 0, :], Y[b][:, 0, t, :], True, False, [E, Y[b]], [p])
                    self.mm(p[:, 0:CB], E[:, t, 1, :], Y[b][:, 1, t, :], False, True, [E, Y[b]], [p])
                    self.cp(fo[b][:, t, :], p[:, 0:CB], [p], [fo[b]], eng="act" if t % 2 else "dve")
                for r0_ in range(0, 128, 16):
                    self.stq(fo[b], dp["f"][o + r0_ * NB:o + (r0_ + 16) * NB, c0:c0 + CB].rearrange("(r t) c -> r t c", t=NB), fo[b][r0_:r0_ + 16, :, :])
            em.barrier()


def _consts(S0, S1):
    c = {}
    c["c_ident"] = np.eye(128, dtype=np.float32).astype(ml_dtypes.bfloat16)
    i = np.arange(128)
    t, l = i[:, None], i[None, :]
    c["c_tri"] = np.stack([(t <= l), (t >= l), (t > l), (t < l)]).astype(np.float32)
    c["c_ones"] = np.ones((128, 128), np.float32)
    c["c_idf"] = np.eye(128, dtype=np.float32)
    S = max(S0, S1)
    inv = (10000.0 ** (-np.arange(0, 32, 2, dtype=np.float32) / 32)).astype(np.float32)
    ang = np.arange(S, dtype=np.float32)[:, None] * inv[None, :]
    c["c_rope"] = np.concatenate([np.cos(ang), np.sin(ang)], axis=1).astype(np.float32)
    cc = np.arange(256)
    th = 2 * np.pi * (cc[:, None] * cc[None, :] % 256) / 256
    c["c_cs"] = (np.concatenate([np.cos(th), -np.sin(th)], axis=1) / 16.0).astype(np.float32)
    for k, Sx in enumerate([S0, S1]):
        NB = Sx // 128
        b = np.arange(NB)
        ph = 2 * np.pi * (b[:, None] * b[None, :] % NB) / NB
        c[f"c_d1_{k}"] = np.stack([np.concatenate([np.cos(ph), -np.sin(ph)], 1), np.concatenate([np.sin(ph), np.cos(ph)], 1)]).astype(np.float32)
        q = np.arange(128, dtype=np.int64)
        m = (NB * np.arange(128)[None, None, :] + np.arange(NB)[None, :, None])
        th = 2 * np.pi * ((q[:, None, None] * m) % Sx) / Sx
        e = np.stack([np.cos(th), np.sin(th)], axis=2) / math.sqrt(Sx)
        c[f"c_e_{k}"] = e.astype(np.float32).astype(ml_dtypes.bfloat16)
    return c


_W = ["norm_mix_pre", "norm_mix_post", "norm_xa_pre", "norm_xa_post", "norm_mem", "xa_wq", "xa_wkv", "xa_wo", "norm_ffn_pre",
      "norm_ffn_post", "ffn_w_gu", "ffn_w_down", "ev_w_in", "ev_conv_w", "ev_conv_b", "ev_a_log_f", "ev_a_log_b", "ev_dt_bias_f",
      "ev_dt_bias_b", "ev_d_skip", "ev_ssm_norm", "ev_q_norm", "ev_w_uq", "ev_kv_norm", "ev_w_ukv", "ev_w_out", "od_w_mix"]


def make_in_maps(inputs, NC):
    xp = np.asarray(inputs["x_prompt"], np.float32)[0]
    xs = np.asarray(inputs["x_sample"], np.float32)
    mp = np.asarray(inputs["mem_prompt"], np.float32)[0]
    ms = np.asarray(inputs["mem_sample"], np.float32)
    S0, S1 = xs.shape[1], xp.shape[0]
    cst = _consts(S0, S1)
    w = {k: np.ascontiguousarray(np.asarray(inputs[k], np.float32)) for k in _W}
    maps = []
    per = S1 // NC
    for c in range(NC):
        m = dict(w)
        m.update(cst)
        m["x"] = np.ascontiguousarray(np.concatenate([xs[c], xp], axis=0))
        nsb = max(1, per // 128)
        m["c_own"] = (S0 + c * per + np.arange(nsb)[None, :] * 128 + np.arange(128)[:, None]).astype(np.uint32)
        m["mem"] = np.ascontiguousarray(np.stack([ms[c], mp]))
        maps.append(m)
    return maps, S0, S1


_NC_CACHE = {}


def kernel(**inputs):
    NC = 8
    maps, S0, S1 = make_in_maps(inputs, NC)
    key = (S0, S1, NC)
    if key not in _NC_CACHE:
        _NC_CACHE[key] = Prog(S0, S1, NC).build()
    nc = _NC_CACHE[key]
    res = run_bass_kernel_spmd(nc, maps, core_ids=list(range(NC)))
    ys = np.stack([np.asarray(r["y0"], np.float32) for r in res.results])
    per = S1 // NC
    yp = np.concatenate([np.asarray(res.results[c]["y1"], np.float32) for c in range(NC)], axis=0)[None]
    return (yp, ys)
```
